# Optimizing a Trainium2 kernel written in Bass

```python
import math
import jax, jax.numpy as jnp
from jax import lax
import numpy as np

D_MODEL = 2048
BATCH = 2
SEQ = 4096
DEPTH = 4

GLA_HEADS = 4
GLA_KEY_DIM = D_MODEL // 2
GLA_VAL_DIM = D_MODEL
GLA_HEAD_K = GLA_KEY_DIM // GLA_HEADS
GLA_HEAD_V = GLA_VAL_DIM // GLA_HEADS
GK_RANK = 16
GATE_LOGIT_NORMALIZER = 16.0
GLA_CHUNK = 64

DIFF_HEAD_DIM = 64
DIFF_HEADS = D_MODEL // (2 * DIFF_HEAD_DIM)
DIFF_QK_DIM = DIFF_HEADS * 2 * DIFF_HEAD_DIM
DIFF_VAL_DIM = DIFF_HEADS * 2 * DIFF_HEAD_DIM
Q_BLOCK = 128

SPLIT_SIZES = (
    GLA_KEY_DIM,
    GLA_KEY_DIM,
    GLA_VAL_DIM,
    GLA_VAL_DIM,
    GK_RANK,
    DIFF_QK_DIM,
    DIFF_QK_DIM,
    DIFF_VAL_DIM,
    DIFF_VAL_DIM,
    D_MODEL,
    D_MODEL,
)
N_IN = 4 * GLA_KEY_DIM // 2 + 2 * GLA_VAL_DIM + GK_RANK + 2 * DIFF_QK_DIM + 2 * DIFF_VAL_DIM + 2 * D_MODEL
EPS = 1e-6

kernel_name = "gla_diffattn_gated_hybrid"


def rmsnorm(x, g):
    xf = x.astype(jnp.float32)
    y = xf * lax.rsqrt(jnp.mean(xf * xf, axis=-1, keepdims=True) + EPS) * g.astype(jnp.float32)
    return y.astype(x.dtype)


def gla_chunked(q, k, v, gk):
    B, S, H, dk = q.shape
    dv = v.shape[-1]
    n = S // GLA_CHUNK

    def to_chunks(t):
        return t.reshape(B, n, GLA_CHUNK, H, t.shape[-1]).transpose(1, 0, 3, 2, 4)

    qc, kc, vc, gc = to_chunks(q * (dk ** -0.5)), to_chunks(k), to_chunks(v), to_chunks(gk)
    causal = jnp.tril(jnp.ones((GLA_CHUNK, GLA_CHUNK), dtype=bool))

    def step(state, inp):
        qi, ki, vi, gi = inp
        b = jnp.cumsum(gi, axis=-2)
        o_inter = jnp.einsum('bhtk,bhkv->bhtv', qi * jnp.exp(b), state)
        rel = b[:, :, :, None, :] - b[:, :, None, :, :]
        decay = jnp.exp(jnp.where(causal[:, :, None], rel, -jnp.inf))
        scores = jnp.einsum('bhtk,bhsk,bhtsk->bhts', qi, ki, decay)
        o = o_inter + jnp.einsum('bhts,bhsv->bhtv', scores, vi)
        b_last = b[:, :, -1:, :]
        state = jnp.exp(b_last[:, :, 0, :])[..., None] * state + jnp.einsum(
            'bhck,bhcv->bhkv', ki * jnp.exp(b_last - b), vi)
        return state, o

    state0 = jnp.zeros((B, H, dk, dv), jnp.float32)
    _, o = lax.scan(step, state0, (qc, kc, vc, gc))
    return o.transpose(1, 0, 3, 2, 4).reshape(B, S, H, dv)


def diff_attention(q, k, v, lam):
    B, S, H, _, d = q.shape
    nb = S // Q_BLOCK
    qb = (q * (d ** -0.5)).reshape(B, nb, Q_BLOCK, H, 2, d).transpose(1, 0, 2, 3, 4, 5)
    kpos = jnp.arange(S)

    def block(args):
        qi, i = args
        s = jnp.einsum('bqhmd,bkhmd->bhmqk', qi, k).astype(jnp.float32)
        qpos = i * Q_BLOCK + jnp.arange(Q_BLOCK)
        mask = kpos[None, :] <= qpos[:, None]
        p = jax.nn.softmax(jnp.where(mask, s, -jnp.inf), axis=-1)
        w = p[:, :, 0] - lam * p[:, :, 1]
        return jnp.einsum('bhqk,bkhv->bqhv', w.astype(v.dtype), v)

    o = lax.map(block, (qb, jnp.arange(nb)))
    return o.transpose(1, 0, 2, 3, 4).reshape(B, S, H, 2 * d)


def setup_inputs(seed: int = 0) -> dict:
    key = jax.random.key(seed)
    ks = jax.random.split(key, 13)
    f32 = jnp.float32
    x = jax.random.normal(ks[0], (BATCH, SEQ, D_MODEL), f32)
    pre_norm_g = 1.0 + 0.02 * jax.random.normal(ks[1], (DEPTH, D_MODEL), f32)
    post_norm_g = 1.0 + 0.02 * jax.random.normal(ks[2], (DEPTH, D_MODEL), f32)
    w_in = jax.random.normal(ks[3], (DEPTH, D_MODEL, N_IN), f32) * D_MODEL ** -0.5
    gla_gk_w2 = jax.random.normal(ks[4], (DEPTH, GK_RANK, GLA_KEY_DIM), f32) * GK_RANK ** -0.5
    gla_gk_b = 0.1 * jax.random.normal(ks[5], (DEPTH, GLA_KEY_DIM), f32)
    gla_norm_g = 1.0 + 0.02 * jax.random.normal(ks[6], (DEPTH, GLA_HEAD_V), f32)
    diff_lambda = 0.1 * jax.random.normal(ks[7], (DEPTH, 4, DIFF_HEAD_DIM), f32)
    diff_norm_g = 1.0 + 0.02 * jax.random.normal(ks[8], (DEPTH, 2 * DIFF_HEAD_DIM), f32)
    w_branch_a = jax.random.normal(ks[9], (DEPTH, GLA_VAL_DIM, D_MODEL), f32) * GLA_VAL_DIM ** -0.5
    w_branch_b = jax.random.normal(ks[10], (DEPTH, DIFF_VAL_DIM, D_MODEL), f32) * DIFF_VAL_DIM ** -0.5
    w_out = jax.random.normal(ks[11], (DEPTH, D_MODEL, D_MODEL), f32) * D_MODEL ** -0.5
    return {"x": x, "pre_norm_g": pre_norm_g, "post_norm_g": post_norm_g, "w_in": w_in,
            "gla_gk_w2": gla_gk_w2, "gla_gk_b": gla_gk_b, "gla_norm_g": gla_norm_g,
            "diff_lambda": diff_lambda, "diff_norm_g": diff_norm_g,
            "w_branch_a": w_branch_a, "w_branch_b": w_branch_b, "w_out": w_out}


def reference(x, pre_norm_g, post_norm_g, w_in, gla_gk_w2, gla_gk_b, gla_norm_g,
              diff_lambda, diff_norm_g, w_branch_a, w_branch_b, w_out):
    B, S, _ = x.shape
    offsets = list(np.cumsum(SPLIT_SIZES)[:-1])
    for l in range(DEPTH):
        h = rmsnorm(x, pre_norm_g[l])
        proj = h @ w_in[l]
        (a_q, a_k, a_v, a_g, a_lr, b_q, b_k, b_v, b_g, m_a, m_b) = jnp.split(proj, offsets, axis=-1)

        gk = jax.nn.log_sigmoid((a_lr @ gla_gk_w2[l] + gla_gk_b[l]).astype(jnp.float32)) / GATE_LOGIT_NORMALIZER
        o_a = gla_chunked(
            a_q.astype(jnp.float32).reshape(B, S, GLA_HEADS, GLA_HEAD_K),
            a_k.astype(jnp.float32).reshape(B, S, GLA_HEADS, GLA_HEAD_K),
            a_v.astype(jnp.float32).reshape(B, S, GLA_HEADS, GLA_HEAD_V),
            gk.reshape(B, S, GLA_HEADS, GLA_HEAD_K)).astype(h.dtype)
        o_a = rmsnorm(o_a, gla_norm_g[l]).reshape(B, S, GLA_VAL_DIM) * jax.nn.silu(a_g)
        y_a = o_a @ w_branch_a[l]

        lam_init = 0.8 - 0.6 * math.exp(-0.3 * l)
        lq1, lk1, lq2, lk2 = [diff_lambda[l, i].astype(jnp.float32) for i in range(4)]
        lam = jnp.exp(jnp.sum(lq1 * lk1)) - jnp.exp(jnp.sum(lq2 * lk2)) + lam_init
        o_b = diff_attention(
            b_q.reshape(B, S, DIFF_HEADS, 2, DIFF_HEAD_DIM),
            b_k.reshape(B, S, DIFF_HEADS, 2, DIFF_HEAD_DIM),
            b_v.reshape(B, S, DIFF_HEADS, 2 * DIFF_HEAD_DIM), lam)
        o_b = rmsnorm(o_b, diff_norm_g[l]) * (1.0 - lam_init)
        o_b = o_b.reshape(B, S, DIFF_VAL_DIM) * jax.nn.silu(b_g)
        y_b = o_b @ w_branch_b[l]

        merged = jax.nn.sigmoid(m_a) * y_a + jax.nn.sigmoid(m_b) * y_b
        out = merged @ w_out[l]
        x = x + rmsnorm(out, post_norm_g[l])
    return x
```

```python
import math
import numpy as np
import ml_dtypes
import concourse.bass as bass
import concourse.mybir as mybir
from concourse.bass_utils import run_bass_kernel_spmd

F32 = mybir.dt.float32
BF16 = mybir.dt.bfloat16
AF = mybir.ActivationFunctionType
ALU = mybir.AluOpType
AX = mybir.AxisListType

DEBUG_STAGE = 9
D = 2048
S = 4096
DEPTH = 4
TL = 1024
EPS = 1e-6
NCORES = 8
GROUPS = [[0, 1, 2, 3], [4, 5, 6, 7]]

OFF_GQ, OFF_GK, OFF_GV, OFF_GG, OFF_LR = 0, 1024, 2048, 4096, 6144
OFF_DQ, OFF_DK, OFF_DV, OFF_DG = 6160, 8208, 10256, 12304
OFF_MA, OFF_MB = 14352, 16400


class KB:
    COMPUTE = ("pe", "act", "dve", "pool")

    def __init__(self, nc):
        self.nc = nc
        self.eng = {"pe": nc.tensor, "act": nc.scalar, "dve": nc.vector, "pool": nc.gpsimd, "sp": nc.sync}
        self.sem = {}
        self.cnt = {}
        self.isdma = {}
        for e in self.COMPUTE:
            self.sem[e] = nc.alloc_semaphore("s_" + e)
            self.cnt[e] = 0
            self.isdma[e] = False
        self.waited = {e: {} for e in self.eng}
        self.lastw = {}
        self.readers = {}
        self.n_inst = 0

    def _dsem(self, key):
        if key[0] == "L" and key[1].isdigit():
            key = "L" + key[2:]
        if key not in self.sem:
            self.sem[key] = self.nc.alloc_semaphore("d_" + key)
            self.cnt[key] = 0
            self.isdma[key] = True
        return key

    @staticmethod
    def _norm(reads, writes):
        pr = [k for k in reads if k.startswith("ps")]
        if pr:
            return [k for k in reads if not k.startswith("ps")], list(writes) + pr
        return reads, writes

    def _collect(self, reads, writes):
        reads, writes = self._norm(reads, writes)
        deps = []
        for k in reads:
            w = self.lastw.get(k)
            if w is not None:
                deps.append(w)
        for k in writes:
            w = self.lastw.get(k)
            if w is not None:
                deps.append(w)
            deps.extend(self.readers.get(k, ()))
        return deps

    def _wait(self, eng, deps):
        need = {}
        for (sk, v) in deps:
            if self.isdma[sk]:
                v = self.cnt[sk]
            elif sk == eng and eng == "pe":
                continue
            if v > need.get(sk, 0):
                need[sk] = v
        for sk, v in need.items():
            if self.waited[eng].get(sk, 0) >= v:
                continue
            self.eng[eng].wait_ge(self.sem[sk], v)
            self.waited[eng][sk] = v

    def _record(self, rec, reads, writes):
        reads, writes = self._norm(reads, writes)
        for k in reads:
            self.readers.setdefault(k, []).append(rec)
        for k in writes:
            self.lastw[k] = rec
            self.readers[k] = []

    def op(self, eng, fn, reads=(), writes=()):
        self._wait(eng, self._collect(reads, writes))
        inst = fn(self.eng[eng])
        self.cnt[eng] += 1
        inst.then_inc(self.sem[eng], 1)
        self._record((eng, self.cnt[eng]), reads, writes)
        self.n_inst += 1

    def mm_group(self, fns, reads=(), writes=()):
        self._wait("pe", self._collect(reads, writes))
        inst = None
        for fn in fns:
            inst = fn(self.eng["pe"])
            self.n_inst += 1
        self.cnt["pe"] += 1
        inst.then_inc(self.sem["pe"], 1)
        self._record(("pe", self.cnt["pe"]), reads, writes)

    def dma(self, q, out, in_, reads=(), writes=(), sem="dma"):
        sk = self._dsem(sem)
        self._wait(q, self._collect(reads, writes))
        inst = self.eng[q].dma_start(out=out, in_=in_)
        self.cnt[sk] += 16
        inst.then_inc(self.sem[sk], 16)
        self._record((sk, self.cnt[sk]), reads, writes)
        self.n_inst += 1

    def collective(self, kind, in_ap, out_ap, reads=(), writes=(), sem="cc"):
        sk = self._dsem(sem)
        self._wait("pool", self._collect(reads, writes))
        inst = self.nc.gpsimd.collective_compute(kind, ALU.bypass, replica_groups=[list(range(NCORES))],
                                                 ins=[in_ap.opt()], outs=[out_ap.opt()])
        self.cnt[sk] += 1
        inst.then_inc(self.sem[sk], 1)
        self._record((sk, self.cnt[sk]), reads, writes)
        self.n_inst += 1

    def barrier(self):
        for e in self.eng:
            for sk in self.sem:
                if sk == e and e == "pe":
                    continue
                v = self.cnt[sk]
                if v > self.waited[e].get(sk, 0):
                    self.eng[e].wait_ge(self.sem[sk], v)
                    self.waited[e][sk] = v
        self.lastw.clear()
        self.readers.clear()

    def finish(self):
        self.barrier()


class Pools:
    def __init__(self, nc):
        self.nc = nc
        self.ps = [nc.alloc_psum_tensor("psb%d" % i, [128, 512], F32) for i in range(8)]
        self.stack = None

    def begin(self):
        import contextlib
        self.stack = contextlib.ExitStack()

    def end(self):
        self.stack.close()
        self.stack = None

    def sb(self, name, shape, dt):
        if self.stack is None:
            return self.nc.alloc_sbuf_tensor(name, list(shape), dt)
        return self.stack.enter_context(self.nc.sbuf_tensor(name, list(shape), dt))


def psk(i):
    return "ps%d" % i


def host_consts():
    s = np.arange(128)[:, None]
    t = np.arange(128)[None, :]
    ident = (s == t).astype(np.float32)
    maskT = (s <= t).astype(np.float32)
    uinc = maskT * np.float32(-1.0 / 16.0)
    ustr = (s > t).astype(np.float32) * np.float32(-1.0 / 16.0)
    return np.ascontiguousarray(np.stack([ident, maskT, uinc, ustr], axis=1))


class Consts:
    def __init__(self, K, P, consts_ap):
        nc = K.nc
        self.cf = P.sb("c_f32", [128, 4, 128], F32)
        self.cb = P.sb("c_bf", [128, 4, 128], BF16)
        K.dma("sp", self.cf.ap(), consts_ap, writes=["c_f32"], sem="const")
        K.dma("pool", self.cb.ap(), consts_ap, writes=["c_bf"], sem="constb")
        self.ident = self.cb.ap()[:, 0, :]
        self.maskT = self.cb.ap()[:, 1, :]
        self.uinc = self.cf.ap()[:, 2, :]
        self.ustr = self.cf.ap()[:, 3, :]


def load_bcast(K, P, name, dram_row_ap, n, sem="const"):
    t = P.sb(name, [128, n], F32)
    K.dma("sp", t.ap(), dram_row_ap.partition_broadcast(128), writes=[name], sem=sem)
    return t


class NormScratch:
    def __init__(self, P, pfx):
        self.junk = P.sb(pfx + "junk", [128, D], BF16)
        self.hbf = P.sb(pfx + "hbf", [128, D], BF16)
        self.ssq = P.sb(pfx + "ssq", [128, 1], F32)
        self.rstd = P.sb(pfx + "rstd", [128, 1], F32)
        self.k = pfx


def emit_rstd(K, ssq_ap, rstd_ap, ssq_key, rstd_key, n):
    K.op("act", lambda e: e.activation(out=rstd_ap, in_=ssq_ap, func=AF.Ln, scale=1.0 / n, bias=EPS),
         reads=[ssq_key], writes=[rstd_key])
    K.op("act", lambda e: e.activation(out=rstd_ap, in_=rstd_ap, func=AF.Exp, scale=-0.5),
         reads=[rstd_key], writes=[rstd_key])


def emit_prenorm_block(K, P, C, ns, x_ap, xkey, gbc, gkey, hT_ap, hkey, col0, psA, psB):
    k = ns.k
    K.op("act", lambda e: e.activation(out=ns.junk.ap(), in_=x_ap, func=AF.Square, accum_out=ns.ssq.ap()),
         reads=[xkey], writes=[k + "junk", k + "ssq"])
    emit_rstd(K, ns.ssq.ap(), ns.rstd.ap(), k + "ssq", k + "rstd", D)
    K.op("dve", lambda e: e.scalar_tensor_tensor(out=ns.hbf.ap(), in0=x_ap, scalar=ns.rstd.ap()[:, 0:1],
                                                  in1=gbc, op0=ALU.mult, op1=ALU.mult),
         reads=[xkey, k + "rstd", gkey], writes=[k + "hbf"])
    for half, pb in ((0, psA), (1, psB)):
        pst = P.ps[pb].ap().bitcast(BF16)
        fns = []
        for cc in range(8):
            c = half * 8 + cc
            fns.append(lambda e, c=c, cc=cc, pst=pst: e.transpose(
                pst[:, cc * 128:(cc + 1) * 128], ns.hbf.ap()[:, c * 128:(c + 1) * 128], C.ident))
        K.mm_group(fns, reads=[k + "hbf", "c_bf"], writes=[psk(pb)])
        src = pst.rearrange("p (c t) -> p c t", c=8)
        dst = hT_ap[:, half * 8:(half + 1) * 8, col0:col0 + 128]
        if half == 0:
            K.op("act", lambda e, dst=dst, src=src: e.copy(dst, src), reads=[psk(pb)], writes=[hkey])
        else:
            K.op("dve", lambda e, dst=dst, src=src: e.tensor_copy(dst, src), reads=[psk(pb)], writes=[hkey])


def emit_phase0(K, P, C, x_ap, g_row_ap, hT_loc_ap, tag="p0", dk=None):
    dk = dk or {}
    gbc = load_bcast(K, P, tag + "gbc", g_row_ap, D)
    ns = NormScratch(P, tag)
    xt = [P.sb(tag + "x%d" % i, [128, D], F32) for i in range(2)]
    hT = P.sb(tag + "hT", [128, 16, TL], BF16)
    for j in range(TL // 128):
        xb = xt[j % 2]
        xk = tag + "x%d" % (j % 2)
        K.dma("sp", xb.ap(), x_ap[j * 128:(j + 1) * 128, :], writes=[xk], sem=xk)
        emit_prenorm_block(K, P, C, ns, xb.ap(), xk, gbc.ap(), tag + "gbc", hT.ap(), tag + "hT", j * 128, 0, 1)
    K.dma("sp", hT_loc_ap.rearrange("(kc p) t -> p kc t", p=128), hT.ap(), reads=[tag + "hT"],
          writes=([dk["hT_loc"]] if "hT_loc" in dk else []), sem="out_hT")


def build_phase0():
    nc = bass.Bass("TRN2", target_bir_lowering=False)
    x = nc.dram_tensor("x", [TL, D], F32, kind="ExternalInput").ap()
    g = nc.dram_tensor("g", [1, D], F32, kind="ExternalInput").ap()
    consts = nc.dram_tensor("consts", [128, 4, 128], F32, kind="ExternalInput").ap()
    hT = nc.dram_tensor("hT", [D, TL], BF16, kind="ExternalOutput").ap()
    K = KB(nc)
    P = Pools(nc)
    C = Consts(K, P, consts)
    emit_phase0(K, P, C, x, g, hT)
    K.finish()
    return nc


def wview(w_ap):
    return w_ap.rearrange("(kc p) n -> p kc n", p=128)


def emit_phase2(K, P, C, tag, oa_src, ob_src, hT_own, xres_in, xres_out, wma, wmb, wba, wbb, wo,
                gpost_row, gnext_row, hT_loc_out, last, dk=None):
    dk = dk or {}
    g_ = lambda n: [dk[n]] if n in dk else []
    oab = P.sb(tag + "oab", [128, 2 * 16 * TL], BF16)
    oaT = oab.ap()[:, 0:16 * TL].rearrange("p (c t) -> p c t", c=16)
    obT = oab.ap()[:, 16 * TL:2 * 16 * TL].rearrange("p (c t) -> p c t", c=16)
    out_sb = oab.ap().bitcast(F32).rearrange("p (b n) -> p b n", b=8)
    hT = P.sb(tag + "hT", [128, 16, TL], BF16)
    mT = P.sb(tag + "mT", [128, 16, TL], BF16)
    wreg = P.sb(tag + "wreg", [128, 16384], BF16)
    wslA = [[wreg.ap()[:, (kind * 2 + s) * 2048:(kind * 2 + s + 1) * 2048].rearrange("p (k n) -> p k n", k=16)
             for s in range(2)] for kind in range(4)]
    keysA = [[tag + "w%d_%d" % (kind, s) for s in range(2)] for kind in range(4)]
    allA = [k_ for kk in keysA for k_ in kk]
    wslB = [wreg.ap()[:, s * 8192:(s + 1) * 8192].rearrange("p (k n) -> p k n", k=16) for s in range(2)]
    keysB = [tag + "wB0", tag + "wB1"]
    allW = allA + keysB
    fC = wreg.ap().bitcast(F32)
    gpost = fC[:, 0:2048]
    gnext = fC[:, 2048:4096]
    xt = [fC[:, 4096:6144], fC[:, 6144:8192]]
    sa = [P.sb(tag + "sa%d" % i, [128, 512], F32) for i in range(2)]
    sb_ = [P.sb(tag + "sb%d" % i, [128, 512], F32) for i in range(2)]
    ns = NormScratch(P, tag)
    yt = [P.sb(tag + "y%d" % i, [128, D], F32) for i in range(2)]
    pssq = P.sb(tag + "pssq", [128, 1], F32)
    prstd = P.sb(tag + "prstd", [128, 1], F32)

    for src in range(4):
        K.dma("sp", oaT[:, 4 * src:4 * src + 4, :], oa_src(src), reads=g_("oa"), writes=[tag + "oaT"], sem=tag + "ld")
        K.dma("sp", obT[:, 4 * src:4 * src + 4, :], ob_src(src), reads=g_("ob"), writes=[tag + "obT"], sem=tag + "ld")
    K.dma("sp", hT.ap(), hT_own.rearrange("(kc p) t -> p kc t", p=128), reads=g_("hT_loc"), writes=[tag + "hT"],
          sem=tag + "ld")

    wsrc = [wview(wba), wview(wbb), wview(wma), wview(wmb)]
    act_in = [oaT, obT, hT.ap(), hT.ap()]
    act_key = [tag + "oaT", tag + "obT", tag + "hT", tag + "hT"]

    for n in range(16):
        s = n % 2
        for kind in range(4):
            wk = keysA[kind][s]
            K.dma("pool", wslA[kind][s], wsrc[kind][:, :, n * 128:(n + 1) * 128], writes=[wk], sem=wk)
        for half in range(2):
            cols = slice(half * 512, (half + 1) * 512)
            banks = [4 * half + i for i in range(4)]
            for kind in range(4):
                wk = keysA[kind][s]
                fns = []
                for kc in range(16):
                    fns.append(lambda e, kind=kind, kc=kc, b=banks[kind]: e.matmul(
                        P.ps[b].ap(), wslA[kind][s][:, kc, :], act_in[kind][:, kc, cols],
                        start=(kc == 0), stop=(kc == 15)))
                K.mm_group(fns, reads=[wk, act_key[kind]], writes=[psk(banks[kind])])
            h = half
            K.op("act", lambda e: e.activation(out=sa[h].ap(), in_=P.ps[banks[2]].ap(), func=AF.Sigmoid),
                 reads=[psk(banks[2])], writes=[tag + "sa%d" % h])
            K.op("act", lambda e: e.activation(out=sb_[h].ap(), in_=P.ps[banks[3]].ap(), func=AF.Sigmoid),
                 reads=[psk(banks[3])], writes=[tag + "sb%d" % h])
            K.op("dve", lambda e: e.tensor_tensor(sa[h].ap(), sa[h].ap(), P.ps[banks[0]].ap(), ALU.mult),
                 reads=[tag + "sa%d" % h, psk(banks[0])], writes=[tag + "sa%d" % h])
            K.op("dve", lambda e: e.tensor_tensor(sb_[h].ap(), sb_[h].ap(), P.ps[banks[1]].ap(), ALU.mult),
                 reads=[tag + "sb%d" % h, psk(banks[1])], writes=[tag + "sb%d" % h])
            K.op("dve", lambda e: e.tensor_tensor(mT.ap()[:, n, cols], sa[h].ap(), sb_[h].ap(), ALU.add),
                 reads=[tag + "sa%d" % h, tag + "sb%d" % h], writes=[tag + "mT"])

    wov = wview(wo)
    ev = 0
    for ng in range(4):
        s = ng % 2
        wk = keysB[s]
        for q4 in range(4):
            K.dma("pool", wslB[s][:, 4 * q4:4 * q4 + 4, :], wov[:, 4 * q4:4 * q4 + 4, ng * 512:(ng + 1) * 512],
                  writes=[wk] + allA, sem=wk)
        for tb in range(8):
            b = (ng * 8 + tb) % 8
            fns = []
            for kc in range(16):
                fns.append(lambda e, kc=kc, b=b, tb=tb: e.matmul(
                    P.ps[b].ap(), mT.ap()[:, kc, tb * 128:(tb + 1) * 128], wslB[s][:, kc, :],
                    start=(kc == 0), stop=(kc == 15)))
            K.mm_group(fns, reads=[wk, tag + "mT"], writes=[psk(b)])
            dst = out_sb[:, tb, ng * 512:(ng + 1) * 512]
            okey = tag + "out%d" % tb
            if ev % 2 == 0:
                K.op("act", lambda e, dst=dst, b=b: e.copy(dst, P.ps[b].ap()), reads=[psk(b)],
                     writes=[okey, tag + "oaT", tag + "obT"])
            else:
                K.op("dve", lambda e, dst=dst, b=b: e.tensor_copy(dst, P.ps[b].ap()), reads=[psk(b)],
                     writes=[okey, tag + "oaT", tag + "obT"])
            ev += 1

    K.dma("sp", gpost, gpost_row.partition_broadcast(128), writes=[tag + "gpost"] + allW, sem=tag + "g")
    if not last:
        K.dma("sp", gnext, gnext_row.partition_broadcast(128), writes=[tag + "gnext"] + allW, sem=tag + "g")
    for tb in range(8):
        i = tb % 2
        okey = tag + "out%d" % tb
        xk = tag + "x%d" % i
        yk = tag + "y%d" % i
        K.dma("sp", xt[i], xres_in[tb * 128:(tb + 1) * 128, :], reads=g_("x_in"), writes=[xk] + allW, sem=xk)
        K.op("act", lambda e: e.activation(out=ns.junk.ap(), in_=out_sb[:, tb, :], func=AF.Square,
                                           accum_out=pssq.ap()),
             reads=[okey], writes=[tag + "junk", tag + "pssq"])
        emit_rstd(K, pssq.ap(), prstd.ap(), tag + "pssq", tag + "prstd", D)
        K.op("dve", lambda e: e.scalar_tensor_tensor(out=yt[i].ap(), in0=out_sb[:, tb, :], scalar=prstd.ap()[:, 0:1],
                                                      in1=gpost, op0=ALU.mult, op1=ALU.mult),
             reads=[okey, tag + "prstd", tag + "gpost"], writes=[yk])
        K.op("dve", lambda e: e.tensor_tensor(yt[i].ap(), yt[i].ap(), xt[i], ALU.add),
             reads=[yk, xk], writes=[yk])
        K.dma("sp", xres_out[tb * 128:(tb + 1) * 128, :], yt[i].ap(), reads=[yk], writes=g_("x_out"),
              sem=tag + "xst%d" % i)
        if not last:
            emit_prenorm_block(K, P, C, ns, yt[i].ap(), yk, gnext, tag + "gnext", hT.ap(), tag + "hT",
                               tb * 128, 0, 1)
    if not last:
        K.dma("sp", hT_loc_out.rearrange("(kc p) t -> p kc t", p=128), hT.ap(), reads=[tag + "hT"],
              writes=g_("hT_loc"), sem=tag + "hTst")


def build_phase2(last):
    nc = bass.Bass("TRN2", target_bir_lowering=False)
    di = lambda n, sh, dt: nc.dram_tensor(n, sh, dt, kind="ExternalInput").ap()
    do = lambda n, sh, dt: nc.dram_tensor(n, sh, dt, kind="ExternalOutput").ap()
    oT_recv = di("oT_recv", [4, 1024, TL], BF16)
    hT_own = di("hT_own", [D, TL], BF16)
    xres = di("xres", [TL, D], F32)
    wma = di("wma", [D, D], F32)
    wmb = di("wmb", [D, D], F32)
    wba = di("wba", [D, D], F32)
    wbb = di("wbb", [D, D], F32)
    wo = di("wo", [D, D], F32)
    gpost = di("gpost", [1, D], F32)
    gnext = di("gnext", [1, D], F32)
    consts = di("consts", [128, 4, 128], F32)
    xout = do("xout", [TL, D], F32)
    hT_loc = do("hT_loc", [D, TL], BF16)
    K = KB(nc)
    P = Pools(nc)
    C = Consts(K, P, consts)
    emit_phase2(K, P, C, "p2",
                lambda src: oT_recv[src, 0:512, :].rearrange("(c p) t -> p c t", p=128),
                lambda src: oT_recv[src, 512:1024, :].rearrange("(c p) t -> p c t", p=128),
                hT_own, xres, xout, wma, wmb, wba, wbb, wo, gpost, gnext, hT_loc, last)
    K.finish()
    return nc


def hT_tile_src(hT_all, tile):
    r, off = tile // 2, (tile % 2) * 512
    return hT_all[r].rearrange("(kc p) t -> p kc t", p=128)[:, :, off:off + 512]


def emit_phase1b(K, P, C, tag, hT_all, wd, lam_row, dnorm_row, ob_dst, lam_init, dk=None):
    dk = dk or {}
    rk_h = [dk["hT"]] if "hT" in dk else []
    wk_o = [dk["ob"]] if "ob" in dk else []
    NT = S // 512
    hts = [P.sb(tag + "ht%d" % i, [128, 16, 512], BF16) for i in range(3)]
    wbs = [P.sb(tag + "wb%d" % i, [128, 16, 512], BF16) for i in range(2)]
    QT = [P.sb(tag + "QT%d" % i, [128, S], BF16) for i in range(2)]
    KT = [P.sb(tag + "KT%d" % i, [128, S], BF16) for i in range(2)]
    V = [P.sb(tag + "V%d" % i, [128, 32, 130], BF16) for i in range(2)]
    SG = [P.sb(tag + "SG%d" % i, [128, 32, 128], BF16) for i in range(2)]
    OT = [P.sb(tag + "OT%d" % i, [128, S], BF16) for i in range(2)]
    PT = [P.sb(tag + "PT%d" % i, [128, 512], BF16) for i in range(3)]
    lamt = P.sb(tag + "lamt", [128, 4, 64], F32)
    lp = P.sb(tag + "lp", [128, 2, 64], F32)
    ls = P.sb(tag + "ls", [128, 2], F32)
    neglam = P.sb(tag + "neglam", [128, 1], F32)
    dng = P.sb(tag + "dng", [128, 128], F32)
    rr = P.sb(tag + "rr", [128, 4], F32)
    av = [P.sb(tag + "av%d" % i, [128, 128], F32) for i in range(4)]
    ov = [P.sb(tag + "ov%d" % i, [128, 128], F32) for i in range(2)]
    junk = P.sb(tag + "junk", [128, 128], BF16)
    ssq = P.sb(tag + "ssq", [128, 1], F32)
    rstd = P.sb(tag + "rstd", [128, 1], F32)
    og = P.sb(tag + "og", [128, 4, 128], BF16)

    K.dma("sp", lamt.ap().rearrange("p a b -> p (a b)"), lam_row.partition_broadcast(128), writes=[tag + "lamt"], sem=tag + "c")
    K.dma("sp", dng.ap(), dnorm_row.partition_broadcast(128), writes=[tag + "dng"], sem=tag + "c")
    K.op("dve", lambda e: e.tensor_tensor(lp.ap()[:, 0, :], lamt.ap()[:, 0, :], lamt.ap()[:, 1, :], ALU.mult),
         reads=[tag + "lamt"], writes=[tag + "lp"])
    K.op("dve", lambda e: e.tensor_tensor(lp.ap()[:, 1, :], lamt.ap()[:, 2, :], lamt.ap()[:, 3, :], ALU.mult),
         reads=[tag + "lamt", tag + "lp"], writes=[tag + "lp"])
    K.op("dve", lambda e: e.reduce_sum(ls.ap(), lp.ap(), axis=AX.X), reads=[tag + "lp"], writes=[tag + "ls"])
    K.op("act", lambda e: e.activation(out=ls.ap(), in_=ls.ap(), func=AF.Exp), reads=[tag + "ls"], writes=[tag + "ls"])
    K.op("dve", lambda e: e.tensor_tensor(neglam.ap(), ls.ap()[:, 1:2], ls.ap()[:, 0:1], ALU.subtract),
         reads=[tag + "ls"], writes=[tag + "neglam"])
    K.op("dve", lambda e: e.tensor_scalar(neglam.ap(), neglam.ap(), -float(lam_init), None, ALU.add),
         reads=[tag + "neglam"], writes=[tag + "neglam"])
    K.op("dve", lambda e: e.tensor_scalar(dng.ap(), dng.ap(), float(1.0 - lam_init), None, ALU.mult),
         reads=[tag + "dng"], writes=[tag + "dng"])
    if DEBUG_STAGE <= -3:
        return
    for i in range(2):
        K.op("dve", lambda e, i=i: e.memset(V[i].ap()[:, :, 128:130], 1.0), writes=[tag + "V%d" % i])
    if DEBUG_STAGE <= -2:
        return

    tcount = 0
    sbank = 0
    for hh in range(4):
        hb = hh % 2
        wk = tag + "wb%d" % hb
        wvw = wd[hh].rearrange("(kc p) n -> p kc n", p=128)
        for q4 in range(4):
            K.dma("pool", wbs[hb].ap()[:, 4 * q4:4 * q4 + 4, :], wvw[:, 4 * q4:4 * q4 + 4, :], writes=[wk], sem=wk)
        kQ, kK, kV, kS, kO = [tag + n + "%d" % hb for n in ("QT", "KT", "V", "SG", "OT")]
        if DEBUG_STAGE <= -1:
            return
        pb = 0
        for tile in range(NT):
            hs = tcount % 3
            tcount += 1
            hk = tag + "ht%d" % hs
            K.dma("sp", hts[hs].ap(), hT_tile_src(hT_all, tile), reads=rk_h, writes=[hk], sem=hk)
            cols = slice(tile * 512, (tile + 1) * 512)
            for which in range(2):
                b = pb % 8
                pb += 1
                fns = [(lambda e, kc=kc, b=b, which=which: e.matmul(
                    P.ps[b].ap(), wbs[hb].ap()[:, kc, which * 128:(which + 1) * 128], hts[hs].ap()[:, kc, :],
                    start=(kc == 0), stop=(kc == 15))) for kc in range(16)]
                K.mm_group(fns, reads=[wk, hk], writes=[psk(b)])
                if which == 0:
                    K.op("act", lambda e, b=b: e.mul(QT[hb].ap()[:, cols], P.ps[b].ap(), 0.125),
                         reads=[psk(b)], writes=[kQ])
                else:
                    K.op("dve", lambda e, b=b: e.tensor_copy(KT[hb].ap()[:, cols], P.ps[b].ap()),
                         reads=[psk(b)], writes=[kK])
            for blk in range(4 if DEBUG_STAGE != -0.5 else 0):
                b = pb % 8
                pb += 1
                gb = tile * 4 + blk
                fns = [(lambda e, kc=kc, b=b, blk=blk: e.matmul(
                    P.ps[b].ap()[:, 0:256], hts[hs].ap()[:, kc, blk * 128:(blk + 1) * 128],
                    wbs[hb].ap()[:, kc, 256:512], start=(kc == 0), stop=(kc == 15))) for kc in range(16)]
                K.mm_group(fns, reads=[wk, hk], writes=[psk(b)])
                K.op("dve", lambda e, b=b, gb=gb: e.tensor_copy(V[hb].ap()[:, gb, 0:128], P.ps[b].ap()[:, 0:128]),
                     reads=[psk(b)], writes=[kV])
                K.op("act", lambda e, b=b, gb=gb: e.activation(out=SG[hb].ap()[:, gb, :], in_=P.ps[b].ap()[:, 128:256],
                                                               func=(AF.Silu if DEBUG_STAGE != -0.25 else AF.Sigmoid)),
                     reads=[psk(b)], writes=[kS])
        for qg in range(8 if DEBUG_STAGE >= 1 else 0):
            for m in range(2):
                A = [3 + 2 * m, 4 + 2 * m]
                for a_ in A:
                    K.op("dve", lambda e, a_=a_: e.memset(P.ps[a_].ap(), 0.0), writes=[psk(a_)])
                rows = slice(m * 64, (m + 1) * 64)
                nkb = 4 * qg + 4
                for kb in range(nkb):
                    qlo = max(0, kb - 4 * qg)
                    ncols = (4 - qlo) * 128
                    q0 = qg * 512 + qlo * 128
                    sb_i = sbank % 3
                    sbank += 1
                    pk = tag + "PT%d" % sb_i
                    K.mm_group([lambda e, sb_i=sb_i, kb=kb, q0=q0, ncols=ncols, rows=rows: e.matmul(
                        P.ps[sb_i].ap()[:, 0:ncols], KT[hb].ap()[rows, kb * 128:(kb + 1) * 128],
                        QT[hb].ap()[rows, q0:q0 + ncols], start=True, stop=True)],
                        reads=[kK, kQ], writes=[psk(sb_i)])
                    K.op("act", lambda e, sb_i=sb_i, ncols=ncols: e.activation(
                        out=PT[sb_i].ap()[:, 0:ncols], in_=P.ps[sb_i].ap()[:, 0:ncols], func=AF.Exp),
                        reads=[psk(sb_i)], writes=[pk])
                    if kb >= 4 * qg:
                        K.op("pool", lambda e, sb_i=sb_i: e.tensor_tensor(
                            PT[sb_i].ap()[:, 0:128], PT[sb_i].ap()[:, 0:128], C.maskT, ALU.mult),
                            reads=[pk, "c_bf"], writes=[pk])
                    fns = []
                    for qi in range(qlo, 4):
                        acc = P.ps[A[qi // 2]].ap()[:, (qi % 2) * 256:(qi % 2) * 256 + 129]
                        fns.append(lambda e, acc=acc, sb_i=sb_i, qi=qi, qlo=qlo, kb=kb: e.matmul(
                            acc, PT[sb_i].ap()[:, (qi - qlo) * 128:(qi - qlo + 1) * 128], V[hb].ap()[:, kb, 0:129],
                            start=False, stop=False, skip_group_check=True))
                    K.mm_group(fns, reads=[pk, kV], writes=[psk(A[0]), psk(A[1])])
                for qi in range(4):
                    accb = A[qi // 2]
                    acc = P.ps[accb].ap()[:, (qi % 2) * 256:(qi % 2) * 256 + 129]
                    rk = tag + "rr%d" % qi
                    ak = tag + "av%d" % qi
                    K.op("dve", lambda e, acc=acc, qi=qi: e.reciprocal(rr.ap()[:, qi:qi + 1], acc[:, 128:129]),
                         reads=[psk(accb)], writes=[rk])
                    if m == 0:
                        K.op("dve", lambda e, acc=acc, qi=qi: e.tensor_scalar(
                            av[qi].ap(), acc[:, 0:128], rr.ap()[:, qi:qi + 1], None, ALU.mult),
                            reads=[psk(accb), rk], writes=[ak])
                    else:
                        oi = qi % 2
                        okk = tag + "ov%d" % oi
                        gq = qg * 4 + qi
                        K.op("dve", lambda e, qi=qi: e.tensor_tensor(
                            rr.ap()[:, qi:qi + 1], rr.ap()[:, qi:qi + 1], neglam.ap(), ALU.mult),
                            reads=[rk, tag + "neglam"], writes=[rk])
                        K.op("dve", lambda e, acc=acc, qi=qi, oi=oi: e.scalar_tensor_tensor(
                            out=ov[oi].ap(), in0=acc[:, 0:128], scalar=rr.ap()[:, qi:qi + 1], in1=av[qi].ap(),
                            op0=ALU.mult, op1=ALU.add),
                            reads=[psk(accb), rk, ak], writes=[okk])
                        K.op("act", lambda e, oi=oi: e.activation(out=junk.ap(), in_=ov[oi].ap(), func=AF.Square,
                                                                  accum_out=ssq.ap()),
                             reads=[okk], writes=[tag + "junk", tag + "ssq"])
                        emit_rstd(K, ssq.ap(), rstd.ap(), tag + "ssq", tag + "rstd", 128)
                        K.op("dve", lambda e, oi=oi: e.scalar_tensor_tensor(
                            out=ov[oi].ap(), in0=ov[oi].ap(), scalar=rstd.ap()[:, 0:1], in1=dng.ap(),
                            op0=ALU.mult, op1=ALU.mult),
                            reads=[okk, tag + "rstd", tag + "dng"], writes=[okk])
                        K.op("pool", lambda e, oi=oi, qi=qi, gq=gq: e.tensor_tensor(
                            og.ap()[:, qi, :], ov[oi].ap(), SG[hb].ap()[:, gq, :], ALU.mult),
                            reads=[okk, kS], writes=[tag + "og"])
                if m == 1:
                    pst = P.ps[7].ap().bitcast(BF16)
                    fns = [(lambda e, qi=qi: e.transpose(pst[:, qi * 128:(qi + 1) * 128], og.ap()[:, qi, :], C.ident))
                           for qi in range(4)]
                    K.mm_group(fns, reads=[tag + "og", "c_bf"], writes=[psk(7)])
                    K.op("act", lambda e, qg=qg: e.copy(OT[hb].ap()[:, qg * 512:(qg + 1) * 512], pst[:, 0:512]),
                         reads=[psk(7)], writes=[kO])
        K.dma("sp", ob_dst(hh), OT[hb].ap().rearrange("p (j t) -> p j t", j=4), reads=[kO], writes=wk_o,
              sem=tag + "ost%d" % hb)


def build_phase1b(lam_init):
    nc = bass.Bass("TRN2", target_bir_lowering=False)
    di = lambda n, sh, dt: nc.dram_tensor(n, sh, dt, kind="ExternalInput").ap()
    do = lambda n, sh, dt: nc.dram_tensor(n, sh, dt, kind="ExternalOutput").ap()
    hT_all = di("hT_all", [4, D, TL], BF16)
    wd = di("wd", [4, D, 512], F32)
    lam_row = di("lam_row", [1, 256], F32)
    dnorm = di("dnorm", [1, 128], F32)
    consts = di("consts", [128, 4, 128], F32)
    oT_send = do("oT_send", [4, 1024, TL], BF16)
    K = KB(nc)
    P = Pools(nc)
    C = Consts(K, P, consts)
    emit_phase1b(K, P, C, "pb", hT_all, wd, lam_row, dnorm,
                 lambda hh: oT_send[:, 512 + hh * 128:512 + (hh + 1) * 128, :].rearrange("j p t -> p j t"), lam_init)
    K.finish()
    return nc


def emit_phase1a(K, P, C, tag, hT_all, wa, w2g, gbias_row, gnorm_row, oa_dst, dk=None):
    dk = dk or {}
    rk_h = [dk["hT"]] if "hT" in dk else []
    wk_o = [dk["oa"]] if "oa" in dk else []
    NT = S // 512
    WA = P.sb(tag + "WA", [128, 16, 1552], BF16)
    hts = [P.sb(tag + "ht%d" % i, [128, 16, 512], BF16) for i in range(3)]
    qTf = [P.sb(tag + "qTf%d" % i, [128, 2, 512], F32) for i in range(2)]
    kTf = [P.sb(tag + "kTf%d" % i, [128, 2, 512], F32) for i in range(2)]
    ktok = [P.sb(tag + "ktok%d" % i, [128, 4, 256], F32) for i in range(2)]
    vtok = [P.sb(tag + "vtok%d" % i, [128, 4, 512], BF16) for i in range(2)]
    sg = [P.sb(tag + "sg%d" % i, [128, 4, 512], BF16) for i in range(2)]
    lrT = [P.sb(tag + "lrT%d" % i, [16, 512], F32) for i in range(2)]
    w2 = P.sb(tag + "w2", [16, 256], F32)
    bias_bc = P.sb(tag + "bias", [128, 256], F32)
    gn_bc = P.sb(tag + "gn", [128, 512], F32)
    st = P.sb(tag + "st", [128, 2, 512], F32)
    stb = P.sb(tag + "stb", [128, 2, 512], BF16)
    zb = P.sb(tag + "zb", [128, 256], F32)
    sp_ = P.sb(tag + "sp", [128, 256], F32)
    ebT = P.sb(tag + "ebT", [128, 2, 128], F32)
    enbT = P.sb(tag + "enbT", [128, 2, 128], F32)
    ebrev = P.sb(tag + "ebrev", [128, 256], F32)
    qtil = P.sb(tag + "qtil", [128, 2, 128], BF16)
    ktil = P.sb(tag + "ktil", [128, 2, 128], BF16)
    khat = P.sb(tag + "khat", [128, 256], BF16)
    scT = P.sb(tag + "scT", [128, 128], BF16)
    on = P.sb(tag + "on", [128, 512], F32)
    ogt = P.sb(tag + "ogt", [128, 512], BF16)
    junk = P.sb(tag + "junk", [128, 512], BF16)
    ssq = P.sb(tag + "ssq", [128, 1], F32)
    rstd = P.sb(tag + "rstd", [128, 1], F32)
    oTa = P.sb(tag + "oTa", [128, 4, S], BF16)

    wav = wa.rearrange("(kc p) n -> p kc n", p=128)
    for q4 in range(4):
        K.dma("pool", WA.ap()[:, 4 * q4:4 * q4 + 4, :], wav[:, 4 * q4:4 * q4 + 4, :], writes=[tag + "WA"], sem=tag + "WA")
    K.dma("sp", w2.ap(), w2g, writes=[tag + "w2"], sem=tag + "c")
    K.dma("sp", bias_bc.ap(), gbias_row.partition_broadcast(128), writes=[tag + "bias"], sem=tag + "c")
    K.dma("sp", gn_bc.ap(), gnorm_row.partition_broadcast(128), writes=[tag + "gn"], sem=tag + "c")
    K.op("dve", lambda e: e.memset(st.ap(), 0.0), writes=[tag + "st"])
    K.op("dve", lambda e: e.memset(stb.ap(), 0.0), writes=[tag + "stb"])

    pb = 0
    for tile in range(NT):
        hs = tile % 3
        hk = tag + "ht%d" % hs
        db = tile % 2
        K.dma("sp", hts[hs].ap(), hT_tile_src(hT_all, tile), reads=rk_h, writes=[hk], sem=hk)
        kq, kk, kkt, kv, ksg, klr = [tag + n + "%d" % db for n in ("qTf", "kTf", "ktok", "vtok", "sg", "lrT")]
        for which in range(4):
            b = pb % 8
            pb += 1
            c0 = which * 128
            fns = [(lambda e, kc=kc, b=b, c0=c0: e.matmul(
                P.ps[b].ap(), WA.ap()[:, kc, c0:c0 + 128], hts[hs].ap()[:, kc, :],
                start=(kc == 0), stop=(kc == 15))) for kc in range(16)]
            K.mm_group(fns, reads=[tag + "WA", hk], writes=[psk(b)])
            dst = (qTf if which < 2 else kTf)[db].ap()[:, which % 2, :]
            dk_ = kq if which < 2 else kk
            if which % 2 == 0:
                K.op("act", lambda e, b=b, dst=dst: e.copy(dst, P.ps[b].ap()), reads=[psk(b)], writes=[dk_])
            else:
                K.op("dve", lambda e, b=b, dst=dst: e.tensor_copy(dst, P.ps[b].ap()), reads=[psk(b)], writes=[dk_])
        b = pb % 8
        pb += 1
        fns = [(lambda e, kc=kc, b=b: e.matmul(
            P.ps[b].ap()[0:16, :], WA.ap()[:, kc, 1536:1552], hts[hs].ap()[:, kc, :],
            start=(kc == 0), stop=(kc == 15))) for kc in range(16)]
        K.mm_group(fns, reads=[tag + "WA", hk], writes=[psk(b)])
        K.op("act", lambda e, b=b: e.copy(lrT[db].ap(), P.ps[b].ap()[0:16, :]), reads=[psk(b)], writes=[klr])
        for blk in range(4):
            tok = slice(blk * 128, (blk + 1) * 128)
            for which in range(3):
                b = pb % 8
                pb += 1
                c0, n = ((256, 256), (512, 512), (1024, 512))[which]
                fns = [(lambda e, kc=kc, b=b, c0=c0, n=n: e.matmul(
                    P.ps[b].ap()[:, 0:n], hts[hs].ap()[:, kc, tok], WA.ap()[:, kc, c0:c0 + n],
                    start=(kc == 0), stop=(kc == 15))) for kc in range(16)]
                K.mm_group(fns, reads=[tag + "WA", hk], writes=[psk(b)])
                if which == 0:
                    K.op("dve", lambda e, b=b: e.tensor_copy(ktok[db].ap()[:, blk, :], P.ps[b].ap()[:, 0:256]),
                         reads=[psk(b)], writes=[kkt])
                elif which == 1:
                    K.op("dve", lambda e, b=b: e.tensor_copy(vtok[db].ap()[:, blk, :], P.ps[b].ap()),
                         reads=[psk(b)], writes=[kv])
                else:
                    K.op("act", lambda e, b=b: e.activation(out=sg[db].ap()[:, blk, :], in_=P.ps[b].ap(), func=AF.Silu),
                         reads=[psk(b)], writes=[ksg])
        for blk in range(4):
            tok = slice(blk * 128, (blk + 1) * 128)
            gchunk = tile * 4 + blk
            K.mm_group([lambda e: e.matmul(P.ps[0].ap()[:, 0:256], lrT[db].ap()[:, tok], w2.ap(), start=True, stop=True)],
                       reads=[klr, tag + "w2"], writes=[psk(0)])
            K.op("dve", lambda e: e.tensor_tensor(zb.ap(), P.ps[0].ap()[:, 0:256], bias_bc.ap(), ALU.add),
                 reads=[psk(0), tag + "bias"], writes=[tag + "zb"])
            K.op("act", lambda e: e.activation(out=zb.ap(), in_=zb.ap(), func=AF.Exp, scale=-1.0),
                 reads=[tag + "zb"], writes=[tag + "zb"])
            K.op("act", lambda e: e.activation(out=sp_.ap(), in_=zb.ap(), func=AF.Ln, bias=1.0),
                 reads=[tag + "zb"], writes=[tag + "sp"])
            K.mm_group([lambda e: e.matmul(P.ps[1].ap()[:, 0:256], C.ustr, sp_.ap(), start=True, stop=True)],
                       reads=[tag + "sp", "c_f32"], writes=[psk(1)])
            K.mm_group([lambda e, c=c: e.matmul(P.ps[2].ap()[:, c * 128:(c + 1) * 128], sp_.ap()[:, c * 128:(c + 1) * 128],
                                                C.uinc, start=True, stop=True) for c in range(2)],
                       reads=[tag + "sp", "c_f32"], writes=[psk(2)])
            K.op("act", lambda e: e.activation(out=ebrev.ap(), in_=P.ps[1].ap()[:, 0:256], func=AF.Exp),
                 reads=[psk(1)], writes=[tag + "ebrev"])
            K.op("act", lambda e: e.activation(out=ebT.ap().rearrange("p c t -> p (c t)"), in_=P.ps[2].ap()[:, 0:256],
                                               func=AF.Exp),
                 reads=[psk(2)], writes=[tag + "ebT"])
            K.op("act", lambda e: e.activation(out=enbT.ap().rearrange("p c t -> p (c t)"), in_=P.ps[2].ap()[:, 0:256],
                                               func=AF.Exp, scale=-1.0),
                 reads=[psk(2)], writes=[tag + "enbT"])
            K.op("dve", lambda e: e.scalar_tensor_tensor(out=qtil.ap(), in0=qTf[db].ap()[:, :, tok], scalar=1.0 / 16.0,
                                                          in1=ebT.ap(), op0=ALU.mult, op1=ALU.mult),
                 reads=[kq, tag + "ebT"], writes=[tag + "qtil"])
            K.op("dve", lambda e: e.tensor_tensor(ktil.ap(), kTf[db].ap()[:, :, tok], enbT.ap(), ALU.mult),
                 reads=[kk, tag + "enbT"], writes=[tag + "ktil"])
            K.op("pool", lambda e: e.tensor_tensor(khat.ap(), ktok[db].ap()[:, blk, :], ebrev.ap(), ALU.mult),
                 reads=[kkt, tag + "ebrev"], writes=[tag + "khat"])
            K.mm_group([lambda e, c=c: e.matmul(P.ps[3].ap()[:, 0:128], ktil.ap()[:, c, :], qtil.ap()[:, c, :],
                                                start=(c == 0), stop=(c == 1)) for c in range(2)],
                       reads=[tag + "ktil", tag + "qtil"], writes=[psk(3)])
            K.op("dve", lambda e: e.tensor_tensor(scT.ap(), P.ps[3].ap()[:, 0:128], C.maskT, ALU.mult),
                 reads=[psk(3), "c_bf"], writes=[tag + "scT"])
            fns = [lambda e, c=c: e.matmul(P.ps[4].ap(), qtil.ap()[:, c, :], stb.ap()[:, c, :],
                                           start=(c == 0), stop=False) for c in range(2)]
            fns.append(lambda e: e.matmul(P.ps[4].ap(), scT.ap(), vtok[db].ap()[:, blk, :], start=False, stop=True))
            K.mm_group(fns, reads=[tag + "qtil", tag + "stb", tag + "scT", kv], writes=[psk(4)])
            for c in range(2):
                K.mm_group([lambda e, c=c: e.matmul(P.ps[5 + c].ap(), khat.ap()[:, c * 128:(c + 1) * 128],
                                                    vtok[db].ap()[:, blk, :], start=True, stop=True)],
                           reads=[tag + "khat", kv], writes=[psk(5 + c)])
                K.op("dve", lambda e, c=c: e.scalar_tensor_tensor(
                    out=st.ap()[:, c, :], in0=st.ap()[:, c, :], scalar=ebT.ap()[:, c, 127:128], in1=P.ps[5 + c].ap(),
                    op0=ALU.mult, op1=ALU.add),
                    reads=[tag + "st", tag + "ebT", psk(5 + c)], writes=[tag + "st"])
            K.op("pool", lambda e: e.tensor_copy(stb.ap(), st.ap()), reads=[tag + "st"], writes=[tag + "stb"])
            K.op("act", lambda e: e.activation(out=junk.ap(), in_=P.ps[4].ap(), func=AF.Square, accum_out=ssq.ap()),
                 reads=[psk(4)], writes=[tag + "junk", tag + "ssq"])
            emit_rstd(K, ssq.ap(), rstd.ap(), tag + "ssq", tag + "rstd", 512)
            K.op("dve", lambda e: e.scalar_tensor_tensor(out=on.ap(), in0=P.ps[4].ap(), scalar=rstd.ap()[:, 0:1],
                                                          in1=gn_bc.ap(), op0=ALU.mult, op1=ALU.mult),
                 reads=[psk(4), tag + "rstd", tag + "gn"], writes=[tag + "on"])
            K.op("pool", lambda e: e.tensor_tensor(ogt.ap(), on.ap(), sg[db].ap()[:, blk, :], ALU.mult),
                 reads=[tag + "on", ksg], writes=[tag + "ogt"])
            pst = P.ps[7].ap().bitcast(BF16)
            fns = [(lambda e, c=c: e.transpose(pst[:, c * 128:(c + 1) * 128], ogt.ap()[:, c * 128:(c + 1) * 128], C.ident))
                   for c in range(4)]
            K.mm_group(fns, reads=[tag + "ogt", "c_bf"], writes=[psk(7)])
            K.op("act", lambda e, gchunk=gchunk: e.copy(oTa.ap()[:, :, gchunk * 128:(gchunk + 1) * 128],
                                                        pst[:, 0:512].rearrange("p (c t) -> p c t", c=4)),
                 reads=[psk(7)], writes=[tag + "oTa"])
        if tile % 2 == 1:
            j = tile // 2
            K.dma("sp", oa_dst(j), oTa.ap()[:, :, j * TL:(j + 1) * TL], reads=[tag + "oTa"], writes=wk_o,
                  sem=tag + "ost")


def build_phase1a():
    nc = bass.Bass("TRN2", target_bir_lowering=False)
    di = lambda n, sh, dt: nc.dram_tensor(n, sh, dt, kind="ExternalInput").ap()
    do = lambda n, sh, dt: nc.dram_tensor(n, sh, dt, kind="ExternalOutput").ap()
    hT_all = di("hT_all", [4, D, TL], BF16)
    wa = di("wa", [D, 1552], F32)
    w2g = di("w2g", [16, 256], F32)
    gbias = di("gbias", [1, 256], F32)
    gnorm = di("gnorm", [1, 512], F32)
    consts = di("consts", [128, 4, 128], F32)
    oT_send = do("oT_send", [4, 1024, TL], BF16)
    K = KB(nc)
    P = Pools(nc)
    C = Consts(K, P, consts)
    emit_phase1a(K, P, C, "pa", hT_all, wa, w2g, gbias, gnorm,
                 lambda j: oT_send[j, 0:512, :].rearrange("(c p) t -> p c t", p=128))
    K.finish()
    return nc


def lam_init_of(l):
    return 0.8 - 0.6 * math.exp(-0.3 * l)


def host_wa(w_in_l, g):
    return np.ascontiguousarray(np.concatenate([
        w_in_l[:, OFF_GQ + g * 256:OFF_GQ + (g + 1) * 256],
        w_in_l[:, OFF_GK + g * 256:OFF_GK + (g + 1) * 256],
        w_in_l[:, OFF_GV + g * 512:OFF_GV + (g + 1) * 512],
        w_in_l[:, OFF_GG + g * 512:OFF_GG + (g + 1) * 512],
        w_in_l[:, OFF_LR:OFF_LR + 16]], axis=1))


def host_wd(w_in_l, g):
    out = np.empty((4, D, 512), np.float32)
    for hh in range(4):
        hd = 4 * g + hh
        for i, off in enumerate((OFF_DQ, OFF_DK, OFF_DV, OFF_DG)):
            out[hh, :, i * 128:(i + 1) * 128] = w_in_l[:, off + hd * 128:off + (hd + 1) * 128]
    return out


def build_phase1(lam_init):
    nc = bass.Bass("TRN2", target_bir_lowering=False)
    di = lambda n, sh, dt: nc.dram_tensor(n, sh, dt, kind="ExternalInput").ap()
    do = lambda n, sh, dt: nc.dram_tensor(n, sh, dt, kind="ExternalOutput").ap()
    hT_all = di("hT_all", [4, D, TL], BF16)
    wa = di("wa", [D, 1552], F32)
    w2g = di("w2g", [16, 256], F32)
    gbias = di("gbias", [1, 256], F32)
    gnorm = di("gnorm", [1, 512], F32)
    wd = di("wd", [4, D, 512], F32)
    lam_row = di("lam_row", [1, 256], F32)
    dnorm = di("dnorm", [1, 128], F32)
    consts = di("consts", [128, 4, 128], F32)
    oT_send = do("oT_send", [4, 1024, TL], BF16)
    K = KB(nc)
    P = Pools(nc)
    C = Consts(K, P, consts)
    P.begin()
    emit_phase1a(K, P, C, "pa", hT_all, wa, w2g, gbias, gnorm,
                 lambda j: oT_send[j, 0:512, :].rearrange("(c p) t -> p c t", p=128))
    K.barrier()
    P.end()
    P.begin()
    emit_phase1b(K, P, C, "pb", hT_all, wd, lam_row, dnorm,
                 lambda hh: oT_send[:, 512 + hh * 128:512 + (hh + 1) * 128, :].rearrange("j p t -> p j t"), lam_init)
    K.barrier()
    P.end()
    K.finish()
    return nc


_PROG_CACHE = {}


def _prog(key, fn):
    if key not in _PROG_CACHE:
        _PROG_CACHE[key] = fn()
    return _PROG_CACHE[key]


def kernel_unfused(x, pre_norm_g, post_norm_g, w_in, gla_gk_w2, gla_gk_b, gla_norm_g,
                   diff_lambda, diff_norm_g, w_branch_a, w_branch_b, w_out):
    f32 = np.float32
    x = np.asarray(x, f32)
    w_in = np.asarray(w_in, f32)
    consts = host_consts()
    cores = list(range(NCORES))
    bg = [(c // 4, c % 4) for c in cores]
    xres = [np.ascontiguousarray(x[b, g * TL:(g + 1) * TL, :]) for (b, g) in bg]

    nc0 = _prog("p0", build_phase0)
    ins = [dict(x=xres[c], g=np.asarray(pre_norm_g[0:1], f32), consts=consts) for c in cores]
    res = run_bass_kernel_spmd(nc0, ins, core_ids=cores)
    hT_loc = [np.asarray(res.results[c]["hT"]) for c in cores]

    for l in range(DEPTH):
        last = l == DEPTH - 1
        hT_all = [np.ascontiguousarray(np.stack([hT_loc[4 * b + r] for r in range(4)], axis=0)) for b in range(2)]
        nc1 = _prog(("p1", l), lambda: build_phase1(lam_init_of(l)))
        ins = []
        for c, (b, g) in enumerate(bg):
            ins.append(dict(
                hT_all=hT_all[b], wa=host_wa(w_in[l], g),
                w2g=np.ascontiguousarray(np.asarray(gla_gk_w2[l], f32)[:, g * 256:(g + 1) * 256]),
                gbias=np.ascontiguousarray(np.asarray(gla_gk_b[l], f32)[None, g * 256:(g + 1) * 256]),
                gnorm=np.asarray(gla_norm_g[l], f32)[None, :],
                wd=host_wd(w_in[l], g),
                lam_row=np.asarray(diff_lambda[l], f32).reshape(1, 256),
                dnorm=np.asarray(diff_norm_g[l], f32).reshape(1, 128),
                consts=consts))
        res = run_bass_kernel_spmd(nc1, ins, core_ids=cores)
        oT_send = [np.asarray(res.results[c]["oT_send"]) for c in cores]
        oT_recv = []
        for c, (b, g) in enumerate(bg):
            oT_recv.append(np.ascontiguousarray(np.stack([oT_send[4 * b + src][g] for src in range(4)], axis=0)))
        nc2 = _prog(("p2", last), lambda: build_phase2(last))
        wma = np.ascontiguousarray(w_in[l][:, OFF_MA:OFF_MA + D])
        wmb = np.ascontiguousarray(w_in[l][:, OFF_MB:OFF_MB + D])
        gnext = np.asarray(pre_norm_g[l + 1:l + 2], f32) if not last else np.asarray(pre_norm_g[0:1], f32)
        ins = []
        for c, (b, g) in enumerate(bg):
            ins.append(dict(
                oT_recv=oT_recv[c], hT_own=hT_loc[c], xres=xres[c], wma=wma, wmb=wmb,
                wba=np.asarray(w_branch_a[l], f32), wbb=np.asarray(w_branch_b[l], f32), wo=np.asarray(w_out[l], f32),
                gpost=np.asarray(post_norm_g[l:l + 1], f32), gnext=gnext, consts=consts))
        res = run_bass_kernel_spmd(nc2, ins, core_ids=cores)
        xres = [np.asarray(res.results[c]["xout"]) for c in cores]
        if not last:
            hT_loc = [np.asarray(res.results[c]["hT_loc"]) for c in cores]

    out = np.empty((2, S, D), f32)
    for c, (b, g) in enumerate(bg):
        out[b, g * TL:(g + 1) * TL, :] = xres[c]
    return out


def build_fused():
    nc = bass.Bass("TRN2", target_bir_lowering=False)
    di = lambda n, sh, dt: nc.dram_tensor(n, sh, dt, kind="ExternalInput").ap()
    x = di("x", [TL, D], F32)
    consts = di("consts", [128, 4, 128], F32)
    pre_g = di("pre_g", [DEPTH, D], F32)
    post_g = di("post_g", [DEPTH, D], F32)
    wa = di("wa", [DEPTH, D, 1552], F32)
    w2g = di("w2g", [DEPTH, 16, 256], F32)
    gbias = di("gbias", [DEPTH, 256], F32)
    gnorm = di("gnorm", [DEPTH, 512], F32)
    wd = di("wd", [DEPTH, 4, D, 512], F32)
    lam = di("lam", [DEPTH, 256], F32)
    dnorm = di("dnorm", [DEPTH, 128], F32)
    wma = di("wma", [DEPTH, D, D], F32)
    wmb = di("wmb", [DEPTH, D, D], F32)
    wba = di("wba", [DEPTH, D, D], F32)
    wbb = di("wbb", [DEPTH, D, D], F32)
    wo = di("wo", [DEPTH, D, D], F32)
    xout = nc.dram_tensor("xout", [TL, D], F32, kind="ExternalOutput").ap()
    hT_loc = nc.dram_tensor("hT_loc", [D, TL], BF16).ap()
    hT_all = nc.dram_tensor("hT_all", [NCORES * D, TL], BF16).ap()
    oa_loc = nc.dram_tensor("oa_loc", [512, S], BF16).ap()
    oa_all = nc.dram_tensor("oa_all", [NCORES * 512, S], BF16).ap()
    ob_loc = nc.dram_tensor("ob_loc", [512, S], BF16).ap()
    ob_all = nc.dram_tensor("ob_all", [NCORES * 512, S], BF16).ap()
    xres = [nc.dram_tensor("xresA", [TL, D], F32).ap(), nc.dram_tensor("xresB", [TL, D], F32).ap()]

    K = KB(nc)
    P = Pools(nc)
    C = Consts(K, P, consts)
    pid = nc.sync.partition_id()
    b4 = (pid // 4) * 4
    gq = pid % 4
    hT_grp = hT_all.rearrange("(r d) t -> r d t", r=NCORES)[bass.ds(b4, 4)]

    def oa_src(src):
        v = oa_all.rearrange("(r f) t -> r f t", r=NCORES)[bass.ds(b4 + src, 1)]
        return v[0].rearrange("(c p) t -> p c t", p=128)[:, :, bass.ds(gq * TL, TL)]

    def ob_src(src):
        v = ob_all.rearrange("(r f) t -> r f t", r=NCORES)[bass.ds(b4 + src, 1)]
        return v[0].rearrange("(c p) t -> p c t", p=128)[:, :, bass.ds(gq * TL, TL)]

    oa_dst = lambda j: oa_loc.rearrange("(c p) t -> p c t", p=128)[:, :, j * TL:(j + 1) * TL]
    ob_dst = lambda hh: ob_loc[hh * 128:(hh + 1) * 128, :].rearrange("p (j t) -> p j t", j=4)
    dk1 = {"hT": "d_hT_all", "oa": "d_oa_loc", "ob": "d_ob_loc"}

    P.begin()
    emit_phase0(K, P, C, x, pre_g[0:1, :], hT_loc, tag="p0", dk={"hT_loc": "d_hT_loc"})
    K.barrier()
    P.end()
    for l in range(DEPTH):
        last = l == DEPTH - 1
        t = "L%d" % l
        K.collective("AllGather", hT_loc, hT_all, reads=["d_hT_loc"], writes=["d_hT_all"], sem="cc_h")
        P.begin()
        emit_phase1a(K, P, C, t + "a", hT_grp, wa[l], w2g[l], gbias[l:l + 1, :], gnorm[l:l + 1, :], oa_dst, dk=dk1)
        K.barrier()
        P.end()
        K.collective("AllGather", oa_loc, oa_all, reads=["d_oa_loc"], writes=["d_oa_all"], sem="cc_a")
        P.begin()
        emit_phase1b(K, P, C, t + "b", hT_grp, wd[l], lam[l:l + 1, :], dnorm[l:l + 1, :], ob_dst, lam_init_of(l), dk=dk1)
        K.barrier()
        P.end()
        K.collective("AllGather", ob_loc, ob_all, reads=["d_ob_loc"], writes=["d_ob_all"], sem="cc_b")
        x_in = x if l == 0 else xres[(l - 1) % 2]
        x_out = xout if last else xres[l % 2]
        dk2 = {"oa": "d_oa_all", "ob": "d_ob_all", "hT_loc": "d_hT_loc",
               "x_in": "d_x%d" % ((l - 1) % 2), "x_out": "d_x%d" % (l % 2)}
        P.begin()
        emit_phase2(K, P, C, t + "c", oa_src, ob_src, hT_loc, x_in, x_out, wma[l], wmb[l], wba[l], wbb[l], wo[l],
                    post_g[l:l + 1, :], pre_g[(l + 1) % DEPTH:(l + 1) % DEPTH + 1, :], hT_loc, last, dk=dk2)
        K.barrier()
        P.end()
    K.finish()
    return nc


def kernel_fused(x, pre_norm_g, post_norm_g, w_in, gla_gk_w2, gla_gk_b, gla_norm_g,
                 diff_lambda, diff_norm_g, w_branch_a, w_branch_b, w_out):
    f32 = np.float32
    x = np.asarray(x, f32)
    w_in = np.asarray(w_in, f32)
    consts = host_consts()
    cores = list(range(NCORES))
    bg = [(c // 4, c % 4) for c in cores]
    nc = _prog("fused", build_fused)
    shared = dict(
        consts=consts, pre_g=np.asarray(pre_norm_g, f32), post_g=np.asarray(post_norm_g, f32),
        gnorm=np.asarray(gla_norm_g, f32), lam=np.asarray(diff_lambda, f32).reshape(DEPTH, 256),
        dnorm=np.asarray(diff_norm_g, f32),
        wma=np.ascontiguousarray(w_in[:, :, OFF_MA:OFF_MA + D]), wmb=np.ascontiguousarray(w_in[:, :, OFF_MB:OFF_MB + D]),
        wba=np.asarray(w_branch_a, f32), wbb=np.asarray(w_branch_b, f32), wo=np.asarray(w_out, f32))
    per_g = []
    for g in range(4):
        per_g.append(dict(
            wa=np.stack([host_wa(w_in[l], g) for l in range(DEPTH)], axis=0),
            wd=np.stack([host_wd(w_in[l], g) for l in range(DEPTH)], axis=0),
            w2g=np.ascontiguousarray(np.asarray(gla_gk_w2, f32)[:, :, g * 256:(g + 1) * 256]),
            gbias=np.ascontiguousarray(np.asarray(gla_gk_b, f32)[:, g * 256:(g + 1) * 256])))
    ins = []
    for c, (b, g) in enumerate(bg):
        d = dict(shared)
        d.update(per_g[g])
        d["x"] = np.ascontiguousarray(x[b, g * TL:(g + 1) * TL, :])
        ins.append(d)
    res = run_bass_kernel_spmd(nc, ins, core_ids=cores)
    out = np.empty((2, S, D), f32)
    for c, (b, g) in enumerate(bg):
        out[b, g * TL:(g + 1) * TL, :] = np.asarray(res.results[c]["xout"])
    return out


def kernel(**inputs):
    return kernel_fused(**inputs)
```

```python
import math
import numpy as np
import ml_dtypes
import concourse.bass as bass
import concourse.mybir as mybir
from concourse.bass_utils import run_bass_kernel_spmd

F32 = mybir.dt.float32
BF16 = mybir.dt.bfloat16
AF = mybir.ActivationFunctionType
ALU = mybir.AluOpType
AX = mybir.AxisListType

DEBUG_STAGE = 9
D = 2048
S = 4096
DEPTH = 4
TL = 1024
EPS = 1e-6
NCORES = 8
GROUPS = [[0, 1, 2, 3], [4, 5, 6, 7]]

OFF_GQ, OFF_GK, OFF_GV, OFF_GG, OFF_LR = 0, 1024, 2048, 4096, 6144
OFF_DQ, OFF_DK, OFF_DV, OFF_DG = 6160, 8208, 10256, 12304
OFF_MA, OFF_MB = 14352, 16400


class KB:
    COMPUTE = ("pe", "act", "dve", "pool")

    def __init__(self, nc):
        self.nc = nc
        self.eng = {"pe": nc.tensor, "act": nc.scalar, "dve": nc.vector, "pool": nc.gpsimd, "sp": nc.sync}
        self.sem = {}
        self.cnt = {}
        self.isdma = {}
        for e in self.COMPUTE:
            self.sem[e] = nc.alloc_semaphore("s_" + e)
            self.cnt[e] = 0
            self.isdma[e] = False
        self.waited = {e: {} for e in self.eng}
        self.lastw = {}
        self.readers = {}
        self.n_inst = 0
        self.prog = {e: [] for e in self.eng}

    def check_deadlock(self):
        val = {sk: 0 for sk in self.sem}
        pc = {e: 0 for e in self.eng}
        progress = True
        while progress:
            progress = False
            for e in self.eng:
                st = self.prog[e]
                while pc[e] < len(st):
                    it = st[pc[e]]
                    if it[0] == "wait":
                        if val[it[1]] >= it[2]:
                            pc[e] += 1
                            progress = True
                        else:
                            break
                    else:
                        val[it[1]] += it[2]
                        pc[e] += 1
                        progress = True
        stuck = {e: (pc[e], len(self.prog[e]), self.prog[e][pc[e]] if pc[e] < len(self.prog[e]) else None) for e in self.eng}
        ok = all(pc[e] == len(self.prog[e]) for e in self.eng)
        return ok, stuck, val

    def _dsem(self, key):
        if key[0] == "L" and key[1].isdigit():
            key = "L" + key[2:]
        if key not in self.sem:
            self.sem[key] = self.nc.alloc_semaphore("d_" + key)
            self.cnt[key] = 0
            self.isdma[key] = True
        return key

    @staticmethod
    def _norm(reads, writes):
        pr = [k for k in reads if k.startswith("ps")]
        if pr:
            return [k for k in reads if not k.startswith("ps")], list(writes) + pr
        return reads, writes

    def _collect(self, reads, writes):
        reads, writes = self._norm(reads, writes)
        deps = []
        for k in reads:
            w = self.lastw.get(k)
            if w is not None:
                deps.append(w)
        for k in writes:
            w = self.lastw.get(k)
            if w is not None:
                deps.append(w)
            deps.extend(self.readers.get(k, ()))
        return deps

    def _wait(self, eng, deps):
        need = {}
        for (sk, v) in deps:
            if self.isdma[sk]:
                v = self.cnt[sk]
            elif sk == eng and eng == "pe":
                continue
            if v > need.get(sk, 0):
                need[sk] = v
        for sk, v in need.items():
            if self.waited[eng].get(sk, 0) >= v:
                continue
            self.eng[eng].wait_ge(self.sem[sk], v)
            self.prog[eng].append(("wait", sk, v))
            self.waited[eng][sk] = v

    def _record(self, rec, reads, writes):
        reads, writes = self._norm(reads, writes)
        for k in reads:
            self.readers.setdefault(k, []).append(rec)
        for k in writes:
            self.lastw[k] = rec
            self.readers[k] = []

    def op(self, eng, fn, reads=(), writes=()):
        self._wait(eng, self._collect(reads, writes))
        inst = fn(self.eng[eng])
        self.cnt[eng] += 1
        inst.then_inc(self.sem[eng], 1)
        self.prog[eng].append(("inc", eng, 1, tuple(writes)))
        self._record((eng, self.cnt[eng]), reads, writes)
        self.n_inst += 1

    def mm_group(self, fns, reads=(), writes=()):
        self._wait("pe", self._collect(reads, writes))
        inst = None
        for fn in fns:
            inst = fn(self.eng["pe"])
            self.n_inst += 1
        self.cnt["pe"] += 1
        inst.then_inc(self.sem["pe"], 1)
        self.prog["pe"].append(("inc", "pe", 1, tuple(writes)))
        self._record(("pe", self.cnt["pe"]), reads, writes)

    def dma(self, q, out, in_, reads=(), writes=(), sem="dma"):
        sk = self._dsem(sem)
        self._wait(q, self._collect(reads, writes))
        inst = self.eng[q].dma_start(out=out, in_=in_)
        self.cnt[sk] += 16
        inst.then_inc(self.sem[sk], 16)
        self.prog[q].append(("inc", sk, 16, tuple(writes)))
        self._record((sk, self.cnt[sk]), reads, writes)
        self.n_inst += 1

    def collective(self, kind, in_ap, out_ap, reads=(), writes=(), sem="cc"):
        sk = self._dsem(sem)
        self._wait("pool", self._collect(reads, writes))
        inst = self.nc.gpsimd.collective_compute(kind, ALU.bypass, replica_groups=[list(range(NCORES))],
                                                 ins=[in_ap.opt()], outs=[out_ap.opt()])
        self.cnt[sk] += 1
        inst.then_inc(self.sem[sk], 1)
        self.prog["pool"].append(("inc", sk, 1, tuple(writes)))
        self._record((sk, self.cnt[sk]), reads, writes)
        self.n_inst += 1

    def barrier(self):
        for e in self.eng:
            for sk in self.sem:
                if sk == e and e == "pe":
                    continue
                v = self.cnt[sk]
                if v > self.waited[e].get(sk, 0):
                    self.eng[e].wait_ge(self.sem[sk], v)
                    self.prog[e].append(("wait", sk, v))
                    self.waited[e][sk] = v
        self.lastw.clear()
        self.readers.clear()

    def finish(self):
        self.barrier()


class Pools:
    def __init__(self, nc):
        self.nc = nc
        self.ps = [nc.alloc_psum_tensor("psb%d" % i, [128, 512], F32) for i in range(8)]
        self.stack = None

    def begin(self):
        import contextlib
        self.stack = contextlib.ExitStack()

    def end(self):
        self.stack.close()
        self.stack = None

    def sb(self, name, shape, dt):
        if self.stack is None:
            return self.nc.alloc_sbuf_tensor(name, list(shape), dt)
        return self.stack.enter_context(self.nc.sbuf_tensor(name, list(shape), dt))


def psk(i):
    return "ps%d" % i


def host_consts():
    s = np.arange(128)[:, None]
    t = np.arange(128)[None, :]
    ident = (s == t).astype(np.float32)
    maskT = (s <= t).astype(np.float32)
    uinc = maskT * np.float32(-1.0 / 16.0)
    ustr = (s > t).astype(np.float32) * np.float32(-1.0 / 16.0)
    return np.ascontiguousarray(np.stack([ident, maskT, uinc, ustr], axis=1))


class Consts:
    def __init__(self, K, P, consts_ap):
        nc = K.nc
        self.cf = P.sb("c_f32", [128, 4, 128], F32)
        self.cb = P.sb("c_bf", [128, 4, 128], BF16)
        K.dma("sp", self.cf.ap(), consts_ap, writes=["c_f32"], sem="const")
        K.dma("pool", self.cb.ap(), consts_ap, writes=["c_bf"], sem="constb")
        self.ident = self.cb.ap()[:, 0, :]
        self.maskT = self.cb.ap()[:, 1, :]
        self.uinc = self.cf.ap()[:, 2, :]
        self.ustr = self.cf.ap()[:, 3, :]


def load_bcast(K, P, name, dram_row_ap, n, sem="const"):
    t = P.sb(name, [128, n], F32)
    K.dma("sp", t.ap(), dram_row_ap.partition_broadcast(128), writes=[name], sem=sem)
    return t


class NormScratch:
    def __init__(self, P, pfx):
        self.junk = P.sb(pfx + "junk", [128, D], BF16)
        self.hbf = P.sb(pfx + "hbf", [128, D], BF16)
        self.ssq = P.sb(pfx + "ssq", [128, 1], F32)
        self.rstd = P.sb(pfx + "rstd", [128, 1], F32)
        self.k = pfx


def emit_rstd(K, ssq_ap, rstd_ap, ssq_key, rstd_key, n):
    K.op("act", lambda e: e.activation(out=rstd_ap, in_=ssq_ap, func=AF.Ln, scale=1.0 / n, bias=EPS),
         reads=[ssq_key], writes=[rstd_key])
    K.op("act", lambda e: e.activation(out=rstd_ap, in_=rstd_ap, func=AF.Exp, scale=-0.5),
         reads=[rstd_key], writes=[rstd_key])


def emit_prenorm_block(K, P, C, ns, x_ap, xkey, gbc, gkey, hT_ap, hkey, col0, psA, psB):
    k = ns.k
    K.op("act", lambda e: e.activation(out=ns.junk.ap(), in_=x_ap, func=AF.Square, accum_out=ns.ssq.ap()),
         reads=[xkey], writes=[k + "junk", k + "ssq"])
    emit_rstd(K, ns.ssq.ap(), ns.rstd.ap(), k + "ssq", k + "rstd", D)
    K.op("dve", lambda e: e.scalar_tensor_tensor(out=ns.hbf.ap(), in0=x_ap, scalar=ns.rstd.ap()[:, 0:1],
                                                  in1=gbc, op0=ALU.mult, op1=ALU.mult),
         reads=[xkey, k + "rstd", gkey], writes=[k + "hbf"])
    for half, pb in ((0, psA), (1, psB)):
        pst = P.ps[pb].ap().bitcast(BF16)
        fns = []
        for cc in range(8):
            c = half * 8 + cc
            fns.append(lambda e, c=c, cc=cc, pst=pst: e.transpose(
                pst[:, cc * 128:(cc + 1) * 128], ns.hbf.ap()[:, c * 128:(c + 1) * 128], C.ident))
        K.mm_group(fns, reads=[k + "hbf", "c_bf"], writes=[psk(pb)])
        src = pst.rearrange("p (c t) -> p c t", c=8)
        dst = hT_ap[:, half * 8:(half + 1) * 8, col0:col0 + 128]
        if half == 0:
            K.op("act", lambda e, dst=dst, src=src: e.copy(dst, src), reads=[psk(pb)], writes=[hkey])
        else:
            K.op("dve", lambda e, dst=dst, src=src: e.tensor_copy(dst, src), reads=[psk(pb)], writes=[hkey])


def emit_phase0(K, P, C, x_ap, g_row_ap, hT_loc_ap, tag="p0", dk=None):
    dk = dk or {}
    gbc = load_bcast(K, P, tag + "gbc", g_row_ap, D)
    ns = NormScratch(P, tag)
    xt = [P.sb(tag + "x%d" % i, [128, D], F32) for i in range(2)]
    hT = P.sb(tag + "hT", [128, 16, TL], BF16)
    for j in range(TL // 128):
        xb = xt[j % 2]
        xk = tag + "x%d" % (j % 2)
        K.dma("sp", xb.ap(), x_ap[j * 128:(j + 1) * 128, :], writes=[xk], sem=xk)
        emit_prenorm_block(K, P, C, ns, xb.ap(), xk, gbc.ap(), tag + "gbc", hT.ap(), tag + "hT", j * 128, 0, 1)
        if isinstance(hT_loc_ap, list) and j % 4 == 3:
            hf = j // 4
            K.dma("sp", hT_loc_ap[hf].rearrange("(kc p) t -> p kc t", p=128), hT.ap()[:, :, hf * 512:(hf + 1) * 512],
                  reads=[tag + "hT"], writes=[dk["hT_loc%d" % hf]], sem="out_hT")
            dk["after_half"](hf)
    if not isinstance(hT_loc_ap, list):
        K.dma("sp", hT_loc_ap.rearrange("(kc p) t -> p kc t", p=128), hT.ap(), reads=[tag + "hT"],
              writes=([dk["hT_loc"]] if "hT_loc" in dk else []), sem="out_hT")


def build_phase0():
    nc = bass.Bass("TRN2", target_bir_lowering=False)
    x = nc.dram_tensor("x", [TL, D], F32, kind="ExternalInput").ap()
    g = nc.dram_tensor("g", [1, D], F32, kind="ExternalInput").ap()
    consts = nc.dram_tensor("consts", [128, 4, 128], F32, kind="ExternalInput").ap()
    hT = nc.dram_tensor("hT", [D, TL], BF16, kind="ExternalOutput").ap()
    K = KB(nc)
    P = Pools(nc)
    C = Consts(K, P, consts)
    emit_phase0(K, P, C, x, g, hT)
    K.finish()
    return nc


def wview(w_ap):
    return w_ap.rearrange("(kc p) n -> p kc n", p=128)


def emit_phase2(K, P, C, tag, oa_src, ob_src, hT_own, xres_in, xres_out, wma, wmb, wba, wbb, wo,
                gpost_row, gnext_row, hT_loc_out, last, dk=None):
    dk = dk or {}
    g_ = lambda n: [dk[n]] if n in dk else []
    oab = P.sb(tag + "oab", [128, 2 * 16 * TL], BF16)
    oaT = oab.ap()[:, 0:16 * TL].rearrange("p (c t) -> p c t", c=16)
    obT = oab.ap()[:, 16 * TL:2 * 16 * TL].rearrange("p (c t) -> p c t", c=16)
    out_sb = oab.ap().bitcast(F32).rearrange("p (b n) -> p b n", b=8)
    hT = P.sb(tag + "hT", [128, 16, TL], BF16)
    mT = P.sb(tag + "mT", [128, 16, TL], BF16)
    wreg = P.sb(tag + "wreg", [128, 16384], BF16)
    wslA = [[wreg.ap()[:, (kind * 2 + s) * 2048:(kind * 2 + s + 1) * 2048].rearrange("p (k n) -> p k n", k=16)
             for s in range(2)] for kind in range(4)]
    keysA = [[tag + "w%d_%d" % (kind, s) for s in range(2)] for kind in range(4)]
    allA = [k_ for kk in keysA for k_ in kk]
    wslB = [wreg.ap()[:, s * 8192:(s + 1) * 8192].rearrange("p (k n) -> p k n", k=16) for s in range(2)]
    keysB = [tag + "wB0", tag + "wB1"]
    allW = allA + keysB
    fC = wreg.ap().bitcast(F32)
    gpost = fC[:, 0:2048]
    gnext = fC[:, 2048:4096]
    xt = [fC[:, 4096:6144], fC[:, 6144:8192]]
    sa = [P.sb(tag + "sa%d" % i, [128, 512], F32) for i in range(2)]
    sb_ = [P.sb(tag + "sb%d" % i, [128, 512], F32) for i in range(2)]
    ns = NormScratch(P, tag)
    yt = [P.sb(tag + "y%d" % i, [128, D], F32) for i in range(2)]
    pssq = P.sb(tag + "pssq", [128, 1], F32)
    prstd = P.sb(tag + "prstd", [128, 1], F32)

    for src in range(4):
        K.dma("sp", oaT[:, 4 * src:4 * src + 4, :], oa_src(src), reads=g_("oa"), writes=[tag + "oaT"], sem=tag + "ld")
        obs = ob_src(src)
        if not isinstance(obs, list):
            obs = [(0, 4, obs, g_("ob"))]
        for (c0, n, ap_, rk_) in obs:
            K.dma("sp", obT[:, 4 * src + c0:4 * src + c0 + n, :] if n > 1 else obT[:, 4 * src + c0, :], ap_, reads=rk_,
                  writes=[tag + "obT"], sem=tag + "ld")
    if isinstance(hT_own, list):
        for hf in range(2):
            K.dma("sp", hT.ap()[:, :, hf * 512:(hf + 1) * 512], hT_own[hf].rearrange("(kc p) t -> p kc t", p=128),
                  reads=g_("hT_loc%d" % hf), writes=[tag + "hT"], sem=tag + "ld")
    else:
        K.dma("sp", hT.ap(), hT_own.rearrange("(kc p) t -> p kc t", p=128), reads=g_("hT_loc"), writes=[tag + "hT"],
              sem=tag + "ld")

    wsrc = [wview(wba), wview(wbb), wview(wma), wview(wmb)]
    act_in = [oaT, obT, hT.ap(), hT.ap()]
    act_key = [tag + "oaT", tag + "obT", tag + "hT", tag + "hT"]

    for n in range(16):
        s = n % 2
        for kind in range(4):
            wk = keysA[kind][s]
            K.dma("pool", wslA[kind][s], wsrc[kind][:, :, n * 128:(n + 1) * 128], writes=[wk], sem=wk)
        for half in range(2):
            cols = slice(half * 512, (half + 1) * 512)
            banks = [4 * half + i for i in range(4)]
            for kind in range(4):
                wk = keysA[kind][s]
                fns = []
                for kc in range(16):
                    fns.append(lambda e, kind=kind, kc=kc, b=banks[kind]: e.matmul(
                        P.ps[b].ap(), wslA[kind][s][:, kc, :], act_in[kind][:, kc, cols],
                        start=(kc == 0), stop=(kc == 15)))
                K.mm_group(fns, reads=[wk, act_key[kind]], writes=[psk(banks[kind])])
            h = half
            K.op("act", lambda e: e.activation(out=sa[h].ap(), in_=P.ps[banks[2]].ap(), func=AF.Sigmoid),
                 reads=[psk(banks[2])], writes=[tag + "sa%d" % h])
            K.op("act", lambda e: e.activation(out=sb_[h].ap(), in_=P.ps[banks[3]].ap(), func=AF.Sigmoid),
                 reads=[psk(banks[3])], writes=[tag + "sb%d" % h])
            K.op("dve", lambda e: e.tensor_tensor(sa[h].ap(), sa[h].ap(), P.ps[banks[0]].ap(), ALU.mult),
                 reads=[tag + "sa%d" % h, psk(banks[0])], writes=[tag + "sa%d" % h])
            K.op("dve", lambda e: e.tensor_tensor(sb_[h].ap(), sb_[h].ap(), P.ps[banks[1]].ap(), ALU.mult),
                 reads=[tag + "sb%d" % h, psk(banks[1])], writes=[tag + "sb%d" % h])
            K.op("dve", lambda e: e.tensor_tensor(mT.ap()[:, n, cols], sa[h].ap(), sb_[h].ap(), ALU.add),
                 reads=[tag + "sa%d" % h, tag + "sb%d" % h], writes=[tag + "mT"])

    wov = wview(wo)
    ev = 0
    for ng in range(4):
        s = ng % 2
        wk = keysB[s]
        for q4 in range(4):
            K.dma("pool", wslB[s][:, 4 * q4:4 * q4 + 4, :], wov[:, 4 * q4:4 * q4 + 4, ng * 512:(ng + 1) * 512],
                  writes=[wk] + allA, sem=wk)
        for tb in range(8):
            b = (ng * 8 + tb) % 8
            fns = []
            for kc in range(16):
                fns.append(lambda e, kc=kc, b=b, tb=tb: e.matmul(
                    P.ps[b].ap(), mT.ap()[:, kc, tb * 128:(tb + 1) * 128], wslB[s][:, kc, :],
                    start=(kc == 0), stop=(kc == 15)))
            K.mm_group(fns, reads=[wk, tag + "mT"], writes=[psk(b)])
            dst = out_sb[:, tb, ng * 512:(ng + 1) * 512]
            okey = tag + "out%d" % tb
            if ev % 2 == 0:
                K.op("act", lambda e, dst=dst, b=b: e.copy(dst, P.ps[b].ap()), reads=[psk(b)],
                     writes=[okey, tag + "oaT", tag + "obT"])
            else:
                K.op("dve", lambda e, dst=dst, b=b: e.tensor_copy(dst, P.ps[b].ap()), reads=[psk(b)],
                     writes=[okey, tag + "oaT", tag + "obT"])
            ev += 1

    K.dma("sp", gpost, gpost_row.partition_broadcast(128), writes=[tag + "gpost"] + allW, sem=tag + "g")
    if not last:
        K.dma("sp", gnext, gnext_row.partition_broadcast(128), writes=[tag + "gnext"] + allW, sem=tag + "g")
    for tb in range(8):
        i = tb % 2
        okey = tag + "out%d" % tb
        xk = tag + "x%d" % i
        yk = tag + "y%d" % i
        K.dma("sp", xt[i], xres_in[tb * 128:(tb + 1) * 128, :], reads=g_("x_in"), writes=[xk] + allW, sem=xk)
        K.op("act", lambda e: e.activation(out=ns.junk.ap(), in_=out_sb[:, tb, :], func=AF.Square,
                                           accum_out=pssq.ap()),
             reads=[okey], writes=[tag + "junk", tag + "pssq"])
        emit_rstd(K, pssq.ap(), prstd.ap(), tag + "pssq", tag + "prstd", D)
        K.op("dve", lambda e: e.scalar_tensor_tensor(out=yt[i].ap(), in0=out_sb[:, tb, :], scalar=prstd.ap()[:, 0:1],
                                                      in1=gpost, op0=ALU.mult, op1=ALU.mult),
             reads=[okey, tag + "prstd", tag + "gpost"], writes=[yk])
        K.op("dve", lambda e: e.tensor_tensor(yt[i].ap(), yt[i].ap(), xt[i], ALU.add),
             reads=[yk, xk], writes=[yk])
        K.dma("sp", xres_out[tb * 128:(tb + 1) * 128, :], yt[i].ap(), reads=[yk], writes=g_("x_out"),
              sem=tag + "xst%d" % i)
        if not last:
            emit_prenorm_block(K, P, C, ns, yt[i].ap(), yk, gnext, tag + "gnext", hT.ap(), tag + "hT",
                               tb * 128, 0, 1)
            if isinstance(hT_loc_out, list) and tb % 4 == 3:
                hf = tb // 4
                K.dma("sp", hT_loc_out[hf].rearrange("(kc p) t -> p kc t", p=128), hT.ap()[:, :, hf * 512:(hf + 1) * 512],
                      reads=[tag + "hT"], writes=g_("hT_loc%d" % hf), sem=tag + "hTst")
                dk["after_half"](hf)
    if not last and not isinstance(hT_loc_out, list):
        K.dma("sp", hT_loc_out.rearrange("(kc p) t -> p kc t", p=128), hT.ap(), reads=[tag + "hT"],
              writes=g_("hT_loc"), sem=tag + "hTst")


def build_phase2(last):
    nc = bass.Bass("TRN2", target_bir_lowering=False)
    di = lambda n, sh, dt: nc.dram_tensor(n, sh, dt, kind="ExternalInput").ap()
    do = lambda n, sh, dt: nc.dram_tensor(n, sh, dt, kind="ExternalOutput").ap()
    oT_recv = di("oT_recv", [4, 1024, TL], BF16)
    hT_own = di("hT_own", [D, TL], BF16)
    xres = di("xres", [TL, D], F32)
    wma = di("wma", [D, D], F32)
    wmb = di("wmb", [D, D], F32)
    wba = di("wba", [D, D], F32)
    wbb = di("wbb", [D, D], F32)
    wo = di("wo", [D, D], F32)
    gpost = di("gpost", [1, D], F32)
    gnext = di("gnext", [1, D], F32)
    consts = di("consts", [128, 4, 128], F32)
    xout = do("xout", [TL, D], F32)
    hT_loc = do("hT_loc", [D, TL], BF16)
    K = KB(nc)
    P = Pools(nc)
    C = Consts(K, P, consts)
    emit_phase2(K, P, C, "p2",
                lambda src: oT_recv[src, 0:512, :].rearrange("(c p) t -> p c t", p=128),
                lambda src: oT_recv[src, 512:1024, :].rearrange("(c p) t -> p c t", p=128),
                hT_own, xres, xout, wma, wmb, wba, wbb, wo, gpost, gnext, hT_loc, last)
    K.finish()
    return nc


def hT_tile_src(hT_all, tile):
    r, off = tile // 2, (tile % 2) * 512
    return hT_all[r].rearrange("(kc p) t -> p kc t", p=128)[:, :, off:off + 512]


def emit_phase1b(K, P, C, tag, hT_all, wd, lam_row, dnorm_row, ob_dst, lam_init, dk=None):
    dk = dk or {}
    hT_src = hT_all if callable(hT_all) else (lambda tile: hT_tile_src(hT_all, tile))
    rk_hf = dk.get("hT", lambda tile: [])
    wk_of = dk.get("ob", lambda hh: [])
    after_head = dk.get("after_head", lambda hh: None)
    NT = S // 512
    hts = [P.sb(tag + "ht%d" % i, [128, 16, 512], BF16) for i in range(3)]
    wbs = [P.sb(tag + "wb%d" % i, [128, 16, 512], BF16) for i in range(2)]
    QT = [P.sb(tag + "QT%d" % i, [128, S], BF16) for i in range(2)]
    KT = [P.sb(tag + "KT%d" % i, [128, S], BF16) for i in range(2)]
    V = [P.sb(tag + "V%d" % i, [128, 32, 130], BF16) for i in range(2)]
    SG = [P.sb(tag + "SG%d" % i, [128, 32, 128], BF16) for i in range(2)]
    OT = [P.sb(tag + "OT%d" % i, [128, S], BF16) for i in range(2)]
    PT = [P.sb(tag + "PT%d" % i, [128, 512], BF16) for i in range(3)]
    lamt = P.sb(tag + "lamt", [128, 4, 64], F32)
    lp = P.sb(tag + "lp", [128, 2, 64], F32)
    ls = P.sb(tag + "ls", [128, 2], F32)
    neglam = P.sb(tag + "neglam", [128, 1], F32)
    dng = P.sb(tag + "dng", [128, 128], F32)
    rr = P.sb(tag + "rr", [128, 8], F32)
    av = [P.sb(tag + "av%d" % i, [128, 128], F32) for i in range(4)]
    ov = [P.sb(tag + "ov%d" % i, [128, 128], F32) for i in range(2)]
    junk = P.sb(tag + "junk", [128, 128], BF16)
    ssq = P.sb(tag + "ssq", [128, 1], F32)
    rstd = P.sb(tag + "rstd", [128, 1], F32)
    og = P.sb(tag + "og", [128, 4, 128], BF16)

    K.dma("sp", lamt.ap().rearrange("p a b -> p (a b)"), lam_row.partition_broadcast(128), writes=[tag + "lamt"], sem=tag + "c")
    K.dma("sp", dng.ap(), dnorm_row.partition_broadcast(128), writes=[tag + "dng"], sem=tag + "c")
    K.op("dve", lambda e: e.tensor_tensor(lp.ap()[:, 0, :], lamt.ap()[:, 0, :], lamt.ap()[:, 1, :], ALU.mult),
         reads=[tag + "lamt"], writes=[tag + "lp"])
    K.op("dve", lambda e: e.tensor_tensor(lp.ap()[:, 1, :], lamt.ap()[:, 2, :], lamt.ap()[:, 3, :], ALU.mult),
         reads=[tag + "lamt", tag + "lp"], writes=[tag + "lp"])
    K.op("dve", lambda e: e.reduce_sum(ls.ap(), lp.ap(), axis=AX.X), reads=[tag + "lp"], writes=[tag + "ls"])
    K.op("act", lambda e: e.activation(out=ls.ap(), in_=ls.ap(), func=AF.Exp), reads=[tag + "ls"], writes=[tag + "ls"])
    K.op("dve", lambda e: e.tensor_tensor(neglam.ap(), ls.ap()[:, 1:2], ls.ap()[:, 0:1], ALU.subtract),
         reads=[tag + "ls"], writes=[tag + "neglam"])
    K.op("dve", lambda e: e.tensor_scalar(neglam.ap(), neglam.ap(), -float(lam_init), None, ALU.add),
         reads=[tag + "neglam"], writes=[tag + "neglam"])
    K.op("dve", lambda e: e.tensor_scalar(dng.ap(), dng.ap(), float(1.0 - lam_init), None, ALU.mult),
         reads=[tag + "dng"], writes=[tag + "dng"])
    if DEBUG_STAGE <= -3:
        return
    for i in range(2):
        K.op("dve", lambda e, i=i: e.memset(V[i].ap()[:, :, 128:130], 1.0), writes=[tag + "V%d" % i])
    if DEBUG_STAGE <= -2:
        return

    tcount = 0
    sbank = 0
    for hh in range(4):
        hb = hh % 2
        wk = tag + "wb%d" % hb
        wvw = wd[hh].rearrange("(kc p) n -> p kc n", p=128)
        for q4 in range(4):
            K.dma("pool", wbs[hb].ap()[:, 4 * q4:4 * q4 + 4, :], wvw[:, 4 * q4:4 * q4 + 4, :], writes=[wk], sem=wk)
        kQ, kK, kV, kS, kO = [tag + n + "%d" % hb for n in ("QT", "KT", "V", "SG", "OT")]
        if DEBUG_STAGE <= -1:
            return
        pb = 0
        for tile in range(NT):
            hs = tcount % 3
            tcount += 1
            hk = tag + "ht%d" % hs
            K.dma("sp", hts[hs].ap(), hT_src(tile), reads=rk_hf(tile), writes=[hk], sem=hk)
            cols = slice(tile * 512, (tile + 1) * 512)
            for which in range(2):
                b = pb % 8
                pb += 1
                fns = [(lambda e, kc=kc, b=b, which=which: e.matmul(
                    P.ps[b].ap(), wbs[hb].ap()[:, kc, which * 128:(which + 1) * 128], hts[hs].ap()[:, kc, :],
                    start=(kc == 0), stop=(kc == 15))) for kc in range(16)]
                K.mm_group(fns, reads=[wk, hk], writes=[psk(b)])
                if which == 0:
                    K.op("act", lambda e, b=b: e.mul(QT[hb].ap()[:, cols], P.ps[b].ap(), 0.125),
                         reads=[psk(b)], writes=[kQ])
                else:
                    K.op("dve", lambda e, b=b: e.tensor_copy(KT[hb].ap()[:, cols], P.ps[b].ap()),
                         reads=[psk(b)], writes=[kK])
            for blk in range(4 if DEBUG_STAGE != -0.5 else 0):
                b = pb % 8
                pb += 1
                gb = tile * 4 + blk
                fns = [(lambda e, kc=kc, b=b, blk=blk: e.matmul(
                    P.ps[b].ap()[:, 0:256], hts[hs].ap()[:, kc, blk * 128:(blk + 1) * 128],
                    wbs[hb].ap()[:, kc, 256:512], start=(kc == 0), stop=(kc == 15))) for kc in range(16)]
                K.mm_group(fns, reads=[wk, hk], writes=[psk(b)])
                K.op("dve", lambda e, b=b, gb=gb: e.tensor_copy(V[hb].ap()[:, gb, 0:128], P.ps[b].ap()[:, 0:128]),
                     reads=[psk(b)], writes=[kV])
                K.op("act", lambda e, b=b, gb=gb: e.activation(out=SG[hb].ap()[:, gb, :], in_=P.ps[b].ap()[:, 128:256],
                                                               func=(AF.Silu if DEBUG_STAGE != -0.25 else AF.Sigmoid)),
                     reads=[psk(b)], writes=[kS])
        pending = []
        LAG = 2

        def push(fn):
            pending.append(fn)
            while len(pending) > LAG:
                pending.pop(0)()

        for qg in range(8 if DEBUG_STAGE >= 1 else 0):
            for m in range(2):
                A = [3 + 2 * m, 4 + 2 * m]
                for a_ in A:
                    K.op("dve", lambda e, a_=a_: e.memset(P.ps[a_].ap(), 0.0), writes=[psk(a_)])
                rows = slice(m * 64, (m + 1) * 64)
                nkb = 4 * qg + 4
                for kb in range(nkb):
                    qlo = max(0, kb - 4 * qg)
                    ncols = (4 - qlo) * 128
                    q0 = qg * 512 + qlo * 128
                    sb_i = sbank % 3
                    sbank += 1
                    pk = tag + "PT%d" % sb_i
                    K.mm_group([lambda e, sb_i=sb_i, kb=kb, q0=q0, ncols=ncols, rows=rows: e.matmul(
                        P.ps[sb_i].ap()[:, 0:ncols], KT[hb].ap()[rows, kb * 128:(kb + 1) * 128],
                        QT[hb].ap()[rows, q0:q0 + ncols], start=True, stop=True)],
                        reads=[kK, kQ], writes=[psk(sb_i)])
                    K.op("act", lambda e, sb_i=sb_i, ncols=ncols: e.activation(
                        out=PT[sb_i].ap()[:, 0:ncols], in_=P.ps[sb_i].ap()[:, 0:ncols], func=AF.Exp),
                        reads=[psk(sb_i)], writes=[pk])
                    if kb >= 4 * qg:
                        K.op("pool", lambda e, sb_i=sb_i: e.tensor_tensor(
                            PT[sb_i].ap()[:, 0:128], PT[sb_i].ap()[:, 0:128], C.maskT, ALU.mult),
                            reads=[pk, "c_bf"], writes=[pk])

                    def pv(sb_i=sb_i, qlo=qlo, kb=kb, A=A, pk=pk):
                        fns = []
                        for qi in range(qlo, 4):
                            acc = P.ps[A[qi // 2]].ap()[:, (qi % 2) * 256:(qi % 2) * 256 + 129]
                            fns.append(lambda e, acc=acc, qi=qi: e.matmul(
                                acc, PT[sb_i].ap()[:, (qi - qlo) * 128:(qi - qlo + 1) * 128], V[hb].ap()[:, kb, 0:129],
                                start=False, stop=False, skip_group_check=True))
                        K.mm_group(fns, reads=[pk, kV], writes=[psk(A[0]), psk(A[1])])
                    push(pv)

                def epilogue(qg=qg, m=m, A=A):
                    for qi in range(4):
                        accb = A[qi // 2]
                        acc = P.ps[accb].ap()[:, (qi % 2) * 256:(qi % 2) * 256 + 129]
                        rk = tag + "rr%d_%d" % (m, qi)
                        ak = tag + "av%d" % qi
                        rcol = rr.ap()[:, m * 4 + qi:m * 4 + qi + 1]
                        K.op("dve", lambda e, acc=acc, rcol=rcol: e.reciprocal(rcol, acc[:, 128:129]),
                             reads=[psk(accb)], writes=[rk])
                        if m == 0:
                            K.op("dve", lambda e, acc=acc, qi=qi, rcol=rcol: e.tensor_scalar(
                                av[qi].ap(), acc[:, 0:128], rcol, None, ALU.mult),
                                reads=[psk(accb), rk], writes=[ak])
                        else:
                            oi = qi % 2
                            okk = tag + "ov%d" % oi
                            gq = qg * 4 + qi
                            K.op("dve", lambda e, rcol=rcol: e.tensor_tensor(rcol, rcol, neglam.ap(), ALU.mult),
                                 reads=[rk, tag + "neglam"], writes=[rk])
                            K.op("dve", lambda e, acc=acc, qi=qi, oi=oi, rcol=rcol: e.scalar_tensor_tensor(
                                out=ov[oi].ap(), in0=acc[:, 0:128], scalar=rcol, in1=av[qi].ap(),
                                op0=ALU.mult, op1=ALU.add),
                                reads=[psk(accb), rk, ak], writes=[okk])
                            K.op("act", lambda e, oi=oi: e.activation(out=junk.ap(), in_=ov[oi].ap(), func=AF.Square,
                                                                      accum_out=ssq.ap()),
                                 reads=[okk], writes=[tag + "junk", tag + "ssq"])
                            emit_rstd(K, ssq.ap(), rstd.ap(), tag + "ssq", tag + "rstd", 128)
                            K.op("dve", lambda e, oi=oi: e.scalar_tensor_tensor(
                                out=ov[oi].ap(), in0=ov[oi].ap(), scalar=rstd.ap()[:, 0:1], in1=dng.ap(),
                                op0=ALU.mult, op1=ALU.mult),
                                reads=[okk, tag + "rstd", tag + "dng"], writes=[okk])
                            K.op("pool", lambda e, oi=oi, qi=qi, gq=gq: e.tensor_tensor(
                                og.ap()[:, qi, :], ov[oi].ap(), SG[hb].ap()[:, gq, :], ALU.mult),
                                reads=[okk, kS], writes=[tag + "og"])
                    if m == 1:
                        pst = P.ps[7].ap().bitcast(BF16)
                        fns = [(lambda e, qi=qi: e.transpose(pst[:, qi * 128:(qi + 1) * 128], og.ap()[:, qi, :], C.ident))
                               for qi in range(4)]
                        K.mm_group(fns, reads=[tag + "og", "c_bf"], writes=[psk(7)])
                        K.op("act", lambda e, qg=qg: e.copy(OT[hb].ap()[:, qg * 512:(qg + 1) * 512], pst[:, 0:512]),
                             reads=[psk(7)], writes=[kO])
                while pending:
                    pending.pop(0)()
                epilogue()
        K.dma("sp", ob_dst(hh), OT[hb].ap().rearrange("p (j t) -> p j t", j=4), reads=[kO], writes=wk_of(hh),
              sem=tag + "ost%d" % hb)
        after_head(hh)


def build_phase1b(lam_init):
    nc = bass.Bass("TRN2", target_bir_lowering=False)
    di = lambda n, sh, dt: nc.dram_tensor(n, sh, dt, kind="ExternalInput").ap()
    do = lambda n, sh, dt: nc.dram_tensor(n, sh, dt, kind="ExternalOutput").ap()
    hT_all = di("hT_all", [4, D, TL], BF16)
    wd = di("wd", [4, D, 512], F32)
    lam_row = di("lam_row", [1, 256], F32)
    dnorm = di("dnorm", [1, 128], F32)
    consts = di("consts", [128, 4, 128], F32)
    oT_send = do("oT_send", [4, 1024, TL], BF16)
    K = KB(nc)
    P = Pools(nc)
    C = Consts(K, P, consts)
    emit_phase1b(K, P, C, "pb", hT_all, wd, lam_row, dnorm,
                 lambda hh: oT_send[:, 512 + hh * 128:512 + (hh + 1) * 128, :].rearrange("j p t -> p j t"), lam_init)
    K.finish()
    return nc


def emit_phase1a(K, P, C, tag, hT_all, wa, w2g, gbias_row, gnorm_row, oa_dst, dk=None):
    dk = dk or {}
    hT_src = hT_all if callable(hT_all) else (lambda tile: hT_tile_src(hT_all, tile))
    rk_hf = dk.get("hT", lambda tile: [])
    wk_o = [dk["oa"]] if "oa" in dk else []
    NT = S // 512
    WA = P.sb(tag + "WA", [128, 16, 1552], BF16)
    hts = [P.sb(tag + "ht%d" % i, [128, 16, 512], BF16) for i in range(2)]
    qTf = [P.sb(tag + "qTf%d" % i, [128, 2, 512], F32) for i in range(2)]
    kTf = [P.sb(tag + "kTf%d" % i, [128, 2, 512], F32) for i in range(2)]
    ktok = [P.sb(tag + "ktok%d" % i, [128, 4, 256], F32) for i in range(2)]
    vtok = [P.sb(tag + "vtok%d" % i, [128, 4, 512], BF16) for i in range(2)]
    sg = [P.sb(tag + "sg%d" % i, [128, 4, 512], BF16) for i in range(2)]
    lrT = [P.sb(tag + "lrT%d" % i, [16, 512], F32) for i in range(2)]
    w2 = P.sb(tag + "w2", [16, 256], F32)
    bias_bc = P.sb(tag + "bias", [128, 2, 256], F32)
    mask4 = P.sb(tag + "mask4", [128, 4, 128], BF16)
    gn_bc = P.sb(tag + "gn", [128, 512], F32)
    st = P.sb(tag + "st", [128, 2, 512], F32)
    stb = P.sb(tag + "stb", [128, 2, 512], BF16)
    zb = P.sb(tag + "zb", [128, 1024], F32)
    sp_ = P.sb(tag + "sp", [128, 1024], F32)
    ebT = P.sb(tag + "ebT", [128, 1024], F32)
    enbT = P.sb(tag + "enbT", [128, 1024], F32)
    ebrev = P.sb(tag + "ebrev", [128, 1024], F32)
    qtil = P.sb(tag + "qtil", [128, 2, 512], BF16)
    ktil = P.sb(tag + "ktil", [128, 2, 512], BF16)
    khat = P.sb(tag + "khat", [128, 1024], BF16)
    scT = P.sb(tag + "scT", [128, 4, 128], BF16)
    on = P.sb(tag + "on", [128, 512], F32)
    ogt = P.sb(tag + "ogt", [128, 512], BF16)
    junk = P.sb(tag + "junk", [128, 512], BF16)
    ssq = P.sb(tag + "ssq", [128, 1], F32)
    rstd = P.sb(tag + "rstd", [128, 1], F32)
    oTa = P.sb(tag + "oTa", [128, 4, S], BF16)

    wav = wa.rearrange("(kc p) n -> p kc n", p=128)
    for q4 in range(4):
        K.dma("pool", WA.ap()[:, 4 * q4:4 * q4 + 4, :], wav[:, 4 * q4:4 * q4 + 4, :], writes=[tag + "WA"], sem=tag + "WA")
    K.dma("sp", w2.ap(), w2g, writes=[tag + "w2"], sem=tag + "c")
    for i in range(2):
        K.dma("sp", bias_bc.ap()[:, i, :], gbias_row.partition_broadcast(128), writes=[tag + "bias"], sem=tag + "c")
    for i in range(4):
        K.op("pool", lambda e, i=i: e.tensor_copy(mask4.ap()[:, i, :], C.maskT), reads=["c_bf"], writes=[tag + "mask4"])
    K.dma("sp", gn_bc.ap(), gnorm_row.partition_broadcast(128), writes=[tag + "gn"], sem=tag + "c")
    K.op("dve", lambda e: e.memset(st.ap(), 0.0), writes=[tag + "st0", tag + "st1"])
    K.op("dve", lambda e: e.memset(stb.ap(), 0.0), writes=[tag + "stb"])

    PB = [2, 3, 6]
    pbc = [0]

    def proj_tasks(tile):
        hs = tile % 2
        hk = tag + "ht%d" % hs
        db = tile % 2
        kq, kk, kkt, kv, ksg, klr = [tag + n + "%d" % db for n in ("qTf", "kTf", "ktok", "vtok", "sg", "lrT")]
        tasks = []

        def load():
            K.dma("sp", hts[hs].ap(), hT_src(tile), reads=rk_hf(tile), writes=[hk], sem=hk)
        tasks.append(load)

        def feat(which):
            b = PB[pbc[0] % 3]
            pbc[0] += 1
            c0 = which * 128
            fns = [(lambda e, kc=kc: e.matmul(
                P.ps[b].ap(), WA.ap()[:, kc, c0:c0 + 128], hts[hs].ap()[:, kc, :],
                start=(kc == 0), stop=(kc == 15))) for kc in range(16)]
            K.mm_group(fns, reads=[tag + "WA", hk], writes=[psk(b)])
            dst = (qTf if which < 2 else kTf)[db].ap()[:, which % 2, :]
            dk_ = kq if which < 2 else kk
            if which % 2 == 0:
                K.op("act", lambda e: e.copy(dst, P.ps[b].ap()), reads=[psk(b)], writes=[dk_])
            else:
                K.op("dve", lambda e: e.tensor_copy(dst, P.ps[b].ap()), reads=[psk(b)], writes=[dk_])

        def lr():
            b = PB[pbc[0] % 3]
            pbc[0] += 1
            fns = [(lambda e, kc=kc: e.matmul(
                P.ps[b].ap()[0:16, :], WA.ap()[:, kc, 1536:1552], hts[hs].ap()[:, kc, :],
                start=(kc == 0), stop=(kc == 15))) for kc in range(16)]
            K.mm_group(fns, reads=[tag + "WA", hk], writes=[psk(b)])
            K.op("act", lambda e: e.copy(lrT[db].ap(), P.ps[b].ap()[0:16, :]), reads=[psk(b)], writes=[klr])

        def tokm(blk, which):
            b = PB[pbc[0] % 3]
            pbc[0] += 1
            tok = slice(blk * 128, (blk + 1) * 128)
            c0, n = ((256, 256), (512, 512), (1024, 512))[which]
            fns = [(lambda e, kc=kc: e.matmul(
                P.ps[b].ap()[:, 0:n], hts[hs].ap()[:, kc, tok], WA.ap()[:, kc, c0:c0 + n],
                start=(kc == 0), stop=(kc == 15))) for kc in range(16)]
            K.mm_group(fns, reads=[tag + "WA", hk], writes=[psk(b)])
            if which == 0:
                K.op("dve", lambda e: e.tensor_copy(ktok[db].ap()[:, blk, :], P.ps[b].ap()[:, 0:256]),
                     reads=[psk(b)], writes=[kkt])
            elif which == 1:
                K.op("dve", lambda e: e.tensor_copy(vtok[db].ap()[:, blk, :], P.ps[b].ap()),
                     reads=[psk(b)], writes=[kv])
            else:
                K.op("act", lambda e: e.activation(out=sg[db].ap()[:, blk, :], in_=P.ps[b].ap(), func=AF.Silu),
                     reads=[psk(b)], writes=[ksg])

        for which in range(4):
            tasks.append(lambda which=which: feat(which))
        tasks.append(lr)
        for blk in range(4):
            for which in range(3):
                tasks.append(lambda blk=blk, which=which: tokm(blk, which))
        return tasks

    for tsk in proj_tasks(0):
        tsk()
    for tile in range(NT):
        db = tile % 2
        kq, kk, kkt, kv, ksg, klr = [tag + n + "%d" % db for n in ("qTf", "kTf", "ktok", "vtok", "sg", "lrT")]
        nxt = proj_tasks(tile + 1) if tile + 1 < NT else []
        for hb2 in range(2):
            K.mm_group([lambda e, blk=blk: e.matmul(P.ps[2 + blk // 2].ap()[:, (blk % 2) * 256:(blk % 2 + 1) * 256],
                                                    lrT[db].ap()[:, blk * 128:(blk + 1) * 128], w2.ap(),
                                                    start=True, stop=True) for blk in (2 * hb2, 2 * hb2 + 1)],
                       reads=[klr, tag + "w2"], writes=[psk(2 + hb2)])
            K.op("dve", lambda e, hb2=hb2: e.tensor_tensor(zb.ap()[:, hb2 * 512:(hb2 + 1) * 512], P.ps[2 + hb2].ap(),
                                                           bias_bc.ap().rearrange("p a b -> p (a b)"), ALU.add),
                 reads=[psk(2 + hb2), tag + "bias"], writes=[tag + "zb"])
        K.op("act", lambda e: e.activation(out=zb.ap(), in_=zb.ap(), func=AF.Exp, scale=-1.0),
             reads=[tag + "zb"], writes=[tag + "zb"])
        K.op("act", lambda e: e.activation(out=sp_.ap(), in_=zb.ap(), func=AF.Ln, bias=1.0),
             reads=[tag + "zb"], writes=[tag + "sp"])
        for hb2 in range(2):
            K.mm_group([lambda e, blk=blk: e.matmul(P.ps[2 + blk // 2].ap()[:, (blk % 2) * 256:(blk % 2 + 1) * 256],
                                                    C.ustr, sp_.ap()[:, blk * 256:(blk + 1) * 256], start=True, stop=True)
                        for blk in (2 * hb2, 2 * hb2 + 1)],
                       reads=[tag + "sp", "c_f32"], writes=[psk(2 + hb2)])
            sl = slice(hb2 * 512, (hb2 + 1) * 512)
            K.op("act", lambda e, hb2=hb2, sl=sl: e.activation(out=ebrev.ap()[:, sl], in_=P.ps[2 + hb2].ap(), func=AF.Exp),
                 reads=[psk(2 + hb2)], writes=[tag + "ebrev"])
        for c in range(2):
            K.mm_group([lambda e, blk=blk, c=c: e.matmul(
                P.ps[4 + c].ap()[:, blk * 128:(blk + 1) * 128],
                sp_.ap()[:, blk * 256 + c * 128:blk * 256 + (c + 1) * 128], C.uinc, start=True, stop=True)
                for blk in range(4)], reads=[tag + "sp", "c_f32"], writes=[psk(4 + c)])
            sl = slice(c * 512, (c + 1) * 512)
            K.op("act", lambda e, c=c, sl=sl: e.activation(out=ebT.ap()[:, sl], in_=P.ps[4 + c].ap(), func=AF.Exp),
                 reads=[psk(4 + c)], writes=[tag + "ebT"])
            K.op("act", lambda e, c=c, sl=sl: e.activation(out=enbT.ap()[:, sl], in_=P.ps[4 + c].ap(), func=AF.Exp,
                                                           scale=-1.0),
                 reads=[psk(4 + c)], writes=[tag + "enbT"])
        K.op("dve", lambda e: e.scalar_tensor_tensor(out=qtil.ap().rearrange("p c t -> p (c t)"),
                                                      in0=qTf[db].ap().rearrange("p c t -> p (c t)"),
                                                      scalar=1.0 / 16.0, in1=ebT.ap(), op0=ALU.mult, op1=ALU.mult),
             reads=[kq, tag + "ebT"], writes=[tag + "qtil"])
        K.op("dve", lambda e: e.tensor_tensor(ktil.ap().rearrange("p c t -> p (c t)"),
                                              kTf[db].ap().rearrange("p c t -> p (c t)"), enbT.ap(), ALU.mult),
             reads=[kk, tag + "enbT"], writes=[tag + "ktil"])
        K.op("pool", lambda e: e.tensor_tensor(khat.ap(), ktok[db].ap().rearrange("p b k -> p (b k)"), ebrev.ap(), ALU.mult),
             reads=[kkt, tag + "ebrev"], writes=[tag + "khat"])
        fns = []
        for blk in range(4):
            for c in range(2):
                fns.append(lambda e, blk=blk, c=c: e.matmul(P.ps[6].ap()[:, blk * 128:(blk + 1) * 128],
                                                            ktil.ap()[:, c, blk * 128:(blk + 1) * 128],
                                                            qtil.ap()[:, c, blk * 128:(blk + 1) * 128],
                                                            start=(c == 0), stop=(c == 1)))
        K.mm_group(fns, reads=[tag + "ktil", tag + "qtil"], writes=[psk(6)])
        K.op("dve", lambda e: e.tensor_tensor(scT.ap().rearrange("p b t -> p (b t)"), P.ps[6].ap(),
                                              mask4.ap().rearrange("p b t -> p (b t)"), ALU.mult),
             reads=[psk(6), tag + "mask4"], writes=[tag + "scT"])
        def epilogue(blk, ob_):
            gchunk = tile * 4 + blk
            K.op("act", lambda e: e.activation(out=junk.ap(), in_=P.ps[ob_].ap(), func=AF.Square, accum_out=ssq.ap()),
                 reads=[psk(ob_)], writes=[tag + "junk", tag + "ssq"])
            emit_rstd(K, ssq.ap(), rstd.ap(), tag + "ssq", tag + "rstd", 512)
            K.op("dve", lambda e: e.scalar_tensor_tensor(out=on.ap(), in0=P.ps[ob_].ap(), scalar=rstd.ap()[:, 0:1],
                                                          in1=gn_bc.ap(), op0=ALU.mult, op1=ALU.mult),
                 reads=[psk(ob_), tag + "rstd", tag + "gn"], writes=[tag + "on"])
            K.op("pool", lambda e: e.tensor_tensor(ogt.ap(), on.ap(), sg[db].ap()[:, blk, :], ALU.mult),
                 reads=[tag + "on", ksg], writes=[tag + "ogt"])
            pst = P.ps[7].ap().bitcast(BF16)
            fns = [(lambda e, c=c: e.transpose(pst[:, c * 128:(c + 1) * 128], ogt.ap()[:, c * 128:(c + 1) * 128], C.ident))
                   for c in range(4)]
            K.mm_group(fns, reads=[tag + "ogt", "c_bf"], writes=[psk(7)])
            K.op("act", lambda e: e.copy(oTa.ap()[:, :, gchunk * 128:(gchunk + 1) * 128],
                                         pst[:, 0:512].rearrange("p (c t) -> p c t", c=4)),
                 reads=[psk(7)], writes=[tag + "oTa"])

        nper = (len(nxt) + 3) // 4
        for blk in range(4):
            ob_ = blk % 2
            fns = [lambda e, c=c: e.matmul(P.ps[ob_].ap(), qtil.ap()[:, c, blk * 128:(blk + 1) * 128],
                                           stb.ap()[:, c, :], start=(c == 0), stop=False) for c in range(2)]
            fns.append(lambda e: e.matmul(P.ps[ob_].ap(), scT.ap()[:, blk, :], vtok[db].ap()[:, blk, :],
                                          start=False, stop=True))
            K.mm_group(fns, reads=[tag + "qtil", tag + "stb", tag + "scT", kv], writes=[psk(ob_)])
            for c in range(2):
                ub = 4 + c
                K.mm_group([lambda e, c=c, ub=ub: e.matmul(
                    P.ps[ub].ap(), khat.ap()[:, blk * 256 + c * 128:blk * 256 + (c + 1) * 128],
                    vtok[db].ap()[:, blk, :], start=True, stop=True)],
                    reads=[tag + "khat", kv], writes=[psk(ub)])
                K.op("dve", lambda e, c=c, ub=ub: e.scalar_tensor_tensor(
                    out=st.ap()[:, c, :], in0=st.ap()[:, c, :],
                    scalar=ebT.ap()[:, c * 512 + blk * 128 + 127:c * 512 + blk * 128 + 128], in1=P.ps[ub].ap(),
                    op0=ALU.mult, op1=ALU.add),
                    reads=[tag + "st%d" % c, tag + "ebT", psk(ub)], writes=[tag + "st%d" % c])
                K.op("dve", lambda e, c=c: e.tensor_copy(stb.ap()[:, c, :], st.ap()[:, c, :]),
                     reads=[tag + "st%d" % c], writes=[tag + "stb"])
            if blk > 0:
                epilogue(blk - 1, (blk - 1) % 2)
            for tsk in nxt[blk * nper:(blk + 1) * nper]:
                tsk()
        epilogue(3, 1)
        if tile % 2 == 1:
            j = tile // 2
            K.dma("sp", oa_dst(j), oTa.ap()[:, :, j * TL:(j + 1) * TL], reads=[tag + "oTa"], writes=wk_o,
                  sem=tag + "ost")


def build_phase1a():
    nc = bass.Bass("TRN2", target_bir_lowering=False)
    di = lambda n, sh, dt: nc.dram_tensor(n, sh, dt, kind="ExternalInput").ap()
    do = lambda n, sh, dt: nc.dram_tensor(n, sh, dt, kind="ExternalOutput").ap()
    hT_all = di("hT_all", [4, D, TL], BF16)
    wa = di("wa", [D, 1552], F32)
    w2g = di("w2g", [16, 256], F32)
    gbias = di("gbias", [1, 256], F32)
    gnorm = di("gnorm", [1, 512], F32)
    consts = di("consts", [128, 4, 128], F32)
    oT_send = do("oT_send", [4, 1024, TL], BF16)
    K = KB(nc)
    P = Pools(nc)
    C = Consts(K, P, consts)
    emit_phase1a(K, P, C, "pa", hT_all, wa, w2g, gbias, gnorm,
                 lambda j: oT_send[j, 0:512, :].rearrange("(c p) t -> p c t", p=128))
    K.finish()
    return nc


def lam_init_of(l):
    return 0.8 - 0.6 * math.exp(-0.3 * l)


def host_wa(w_in_l, g):
    return np.ascontiguousarray(np.concatenate([
        w_in_l[:, OFF_GQ + g * 256:OFF_GQ + (g + 1) * 256],
        w_in_l[:, OFF_GK + g * 256:OFF_GK + (g + 1) * 256],
        w_in_l[:, OFF_GV + g * 512:OFF_GV + (g + 1) * 512],
        w_in_l[:, OFF_GG + g * 512:OFF_GG + (g + 1) * 512],
        w_in_l[:, OFF_LR:OFF_LR + 16]], axis=1))


def host_wd(w_in_l, g):
    out = np.empty((4, D, 512), np.float32)
    for hh in range(4):
        hd = 4 * g + hh
        for i, off in enumerate((OFF_DQ, OFF_DK, OFF_DV, OFF_DG)):
            out[hh, :, i * 128:(i + 1) * 128] = w_in_l[:, off + hd * 128:off + (hd + 1) * 128]
    return out


def build_phase1(lam_init):
    nc = bass.Bass("TRN2", target_bir_lowering=False)
    di = lambda n, sh, dt: nc.dram_tensor(n, sh, dt, kind="ExternalInput").ap()
    do = lambda n, sh, dt: nc.dram_tensor(n, sh, dt, kind="ExternalOutput").ap()
    hT_all = di("hT_all", [4, D, TL], BF16)
    wa = di("wa", [D, 1552], F32)
    w2g = di("w2g", [16, 256], F32)
    gbias = di("gbias", [1, 256], F32)
    gnorm = di("gnorm", [1, 512], F32)
    wd = di("wd", [4, D, 512], F32)
    lam_row = di("lam_row", [1, 256], F32)
    dnorm = di("dnorm", [1, 128], F32)
    consts = di("consts", [128, 4, 128], F32)
    oT_send = do("oT_send", [4, 1024, TL], BF16)
    K = KB(nc)
    P = Pools(nc)
    C = Consts(K, P, consts)
    P.begin()
    emit_phase1a(K, P, C, "pa", hT_all, wa, w2g, gbias, gnorm,
                 lambda j: oT_send[j, 0:512, :].rearrange("(c p) t -> p c t", p=128))
    K.barrier()
    P.end()
    P.begin()
    emit_phase1b(K, P, C, "pb", hT_all, wd, lam_row, dnorm,
                 lambda hh: oT_send[:, 512 + hh * 128:512 + (hh + 1) * 128, :].rearrange("j p t -> p j t"), lam_init)
    K.barrier()
    P.end()
    K.finish()
    return nc


_PROG_CACHE = {}


def _prog(key, fn):
    if key not in _PROG_CACHE:
        _PROG_CACHE[key] = fn()
    return _PROG_CACHE[key]


def kernel_unfused(x, pre_norm_g, post_norm_g, w_in, gla_gk_w2, gla_gk_b, gla_norm_g,
                   diff_lambda, diff_norm_g, w_branch_a, w_branch_b, w_out):
    f32 = np.float32
    x = np.asarray(x, f32)
    w_in = np.asarray(w_in, f32)
    consts = host_consts()
    cores = list(range(NCORES))
    bg = [(c // 4, c % 4) for c in cores]
    xres = [np.ascontiguousarray(x[b, g * TL:(g + 1) * TL, :]) for (b, g) in bg]

    nc0 = _prog("p0", build_phase0)
    ins = [dict(x=xres[c], g=np.asarray(pre_norm_g[0:1], f32), consts=consts) for c in cores]
    res = run_bass_kernel_spmd(nc0, ins, core_ids=cores)
    hT_loc = [np.asarray(res.results[c]["hT"]) for c in cores]

    for l in range(DEPTH):
        last = l == DEPTH - 1
        hT_all = [np.ascontiguousarray(np.stack([hT_loc[4 * b + r] for r in range(4)], axis=0)) for b in range(2)]
        nc1 = _prog(("p1", l), lambda: build_phase1(lam_init_of(l)))
        ins = []
        for c, (b, g) in enumerate(bg):
            ins.append(dict(
                hT_all=hT_all[b], wa=host_wa(w_in[l], g),
                w2g=np.ascontiguousarray(np.asarray(gla_gk_w2[l], f32)[:, g * 256:(g + 1) * 256]),
                gbias=np.ascontiguousarray(np.asarray(gla_gk_b[l], f32)[None, g * 256:(g + 1) * 256]),
                gnorm=np.asarray(gla_norm_g[l], f32)[None, :],
                wd=host_wd(w_in[l], g),
                lam_row=np.asarray(diff_lambda[l], f32).reshape(1, 256),
                dnorm=np.asarray(diff_norm_g[l], f32).reshape(1, 128),
                consts=consts))
        res = run_bass_kernel_spmd(nc1, ins, core_ids=cores)
        oT_send = [np.asarray(res.results[c]["oT_send"]) for c in cores]
        oT_recv = []
        for c, (b, g) in enumerate(bg):
            oT_recv.append(np.ascontiguousarray(np.stack([oT_send[4 * b + src][g] for src in range(4)], axis=0)))
        nc2 = _prog(("p2", last), lambda: build_phase2(last))
        wma = np.ascontiguousarray(w_in[l][:, OFF_MA:OFF_MA + D])
        wmb = np.ascontiguousarray(w_in[l][:, OFF_MB:OFF_MB + D])
        gnext = np.asarray(pre_norm_g[l + 1:l + 2], f32) if not last else np.asarray(pre_norm_g[0:1], f32)
        ins = []
        for c, (b, g) in enumerate(bg):
            ins.append(dict(
                oT_recv=oT_recv[c], hT_own=hT_loc[c], xres=xres[c], wma=wma, wmb=wmb,
                wba=np.asarray(w_branch_a[l], f32), wbb=np.asarray(w_branch_b[l], f32), wo=np.asarray(w_out[l], f32),
                gpost=np.asarray(post_norm_g[l:l + 1], f32), gnext=gnext, consts=consts))
        res = run_bass_kernel_spmd(nc2, ins, core_ids=cores)
        xres = [np.asarray(res.results[c]["xout"]) for c in cores]
        if not last:
            hT_loc = [np.asarray(res.results[c]["hT_loc"]) for c in cores]

    out = np.empty((2, S, D), f32)
    for c, (b, g) in enumerate(bg):
        out[b, g * TL:(g + 1) * TL, :] = xres[c]
    return out


def build_fused():
    nc = bass.Bass("TRN2", target_bir_lowering=False)
    di = lambda n, sh, dt: nc.dram_tensor(n, sh, dt, kind="ExternalInput").ap()
    x = di("x", [TL, D], F32)
    consts = di("consts", [128, 4, 128], F32)
    pre_g = di("pre_g", [DEPTH, D], F32)
    post_g = di("post_g", [DEPTH, D], F32)
    wa = di("wa", [DEPTH, D, 1552], F32)
    w2g = di("w2g", [DEPTH, 16, 256], F32)
    gbias = di("gbias", [DEPTH, 256], F32)
    gnorm = di("gnorm", [DEPTH, 512], F32)
    wd = di("wd", [DEPTH, 4, D, 512], F32)
    lam = di("lam", [DEPTH, 256], F32)
    dnorm = di("dnorm", [DEPTH, 128], F32)
    wma = di("wma", [DEPTH, D, D], F32)
    wmb = di("wmb", [DEPTH, D, D], F32)
    wba = di("wba", [DEPTH, D, D], F32)
    wbb = di("wbb", [DEPTH, D, D], F32)
    wo = di("wo", [DEPTH, D, D], F32)
    xout = nc.dram_tensor("xout", [TL, D], F32, kind="ExternalOutput").ap()
    hT_loc = [nc.dram_tensor("hT_loc%d" % i, [D, 512], BF16).ap() for i in range(2)]
    hT_all = [nc.dram_tensor("hT_all%d" % i, [NCORES * D, 512], BF16).ap() for i in range(2)]
    oa_loc = nc.dram_tensor("oa_loc", [512, S], BF16).ap()
    oa_all = nc.dram_tensor("oa_all", [NCORES * 512, S], BF16).ap()
    ob_loc = [nc.dram_tensor("ob_loc%d" % i, [128, S], BF16).ap() for i in range(4)]
    ob_all = [nc.dram_tensor("ob_all%d" % i, [NCORES * 128, S], BF16).ap() for i in range(4)]
    xres = [nc.dram_tensor("xresA", [TL, D], F32).ap(), nc.dram_tensor("xresB", [TL, D], F32).ap()]

    K = KB(nc)
    P = Pools(nc)
    C = Consts(K, P, consts)
    pid = nc.sync.partition_id()
    b4 = (pid // 4) * 4
    gq = pid % 4

    def hT_src(tile):
        r, hf = tile // 2, tile % 2
        v = hT_all[hf].rearrange("(r d) t -> r d t", r=NCORES)[bass.ds(b4 + r, 1)]
        return v[0].rearrange("(kc p) t -> p kc t", p=128)

    def oa_src(src):
        v = oa_all.rearrange("(r f) t -> r f t", r=NCORES)[bass.ds(b4 + src, 1)]
        return v[0].rearrange("(c p) t -> p c t", p=128)[:, :, bass.ds(gq * TL, TL)]

    def ob_src(src):
        res = []
        for hh in range(4):
            v = ob_all[hh].rearrange("(r f) t -> r f t", r=NCORES)[bass.ds(b4 + src, 1)]
            res.append((hh, 1, v[0][:, bass.ds(gq * TL, TL)], ["d_ob_all%d" % hh]))
        return res

    def after_half(hf):
        K.collective("AllGather", hT_loc[hf], hT_all[hf], reads=["d_hT_loc%d" % hf], writes=["d_hT_all%d" % hf],
                     sem="cc_h")

    def after_head(hh):
        K.collective("AllGather", ob_loc[hh], ob_all[hh], reads=["d_ob_loc%d" % hh], writes=["d_ob_all%d" % hh],
                     sem="cc_b")

    oa_dst = lambda j: oa_loc.rearrange("(c p) t -> p c t", p=128)[:, :, j * TL:(j + 1) * TL]
    ob_dst = lambda hh: ob_loc[hh].rearrange("p (j t) -> p j t", j=4)
    dk1 = {"hT": (lambda tile: ["d_hT_all%d" % (tile % 2)]), "oa": "d_oa_loc",
           "ob": (lambda hh: ["d_ob_loc%d" % hh]), "after_head": after_head}

    P.begin()
    emit_phase0(K, P, C, x, pre_g[0:1, :], hT_loc, tag="p0",
                dk={"hT_loc0": "d_hT_loc0", "hT_loc1": "d_hT_loc1", "after_half": after_half})
    K.barrier()
    P.end()
    for l in range(DEPTH):
        last = l == DEPTH - 1
        t = "L%d" % l
        P.begin()
        emit_phase1a(K, P, C, t + "a", hT_src, wa[l], w2g[l], gbias[l:l + 1, :], gnorm[l:l + 1, :], oa_dst, dk=dk1)
        K.barrier()
        P.end()
        K.collective("AllGather", oa_loc, oa_all, reads=["d_oa_loc"], writes=["d_oa_all"], sem="cc_a")
        P.begin()
        emit_phase1b(K, P, C, t + "b", hT_src, wd[l], lam[l:l + 1, :], dnorm[l:l + 1, :], ob_dst, lam_init_of(l), dk=dk1)
        K.barrier()
        P.end()
        x_in = x if l == 0 else xres[(l - 1) % 2]
        x_out = xout if last else xres[l % 2]
        dk2 = {"oa": "d_oa_all", "hT_loc0": "d_hT_loc0", "hT_loc1": "d_hT_loc1",
               "x_in": "d_x%d" % ((l - 1) % 2), "x_out": "d_x%d" % (l % 2), "after_half": after_half}
        P.begin()
        emit_phase2(K, P, C, t + "c", oa_src, ob_src, hT_loc, x_in, x_out, wma[l], wmb[l], wba[l], wbb[l], wo[l],
                    post_g[l:l + 1, :], pre_g[(l + 1) % DEPTH:(l + 1) % DEPTH + 1, :], hT_loc, last, dk=dk2)
        K.barrier()
        P.end()
    ok, stuck, _ = K.check_deadlock()
    assert ok, stuck
    K.finish()
    return nc


def kernel_fused(x, pre_norm_g, post_norm_g, w_in, gla_gk_w2, gla_gk_b, gla_norm_g,
                 diff_lambda, diff_norm_g, w_branch_a, w_branch_b, w_out):
    f32 = np.float32
    x = np.asarray(x, f32)
    w_in = np.asarray(w_in, f32)
    consts = host_consts()
    cores = list(range(NCORES))
    bg = [(c // 4, c % 4) for c in cores]
    nc = _prog("fused", build_fused)
    shared = dict(
        consts=consts, pre_g=np.asarray(pre_norm_g, f32), post_g=np.asarray(post_norm_g, f32),
        gnorm=np.asarray(gla_norm_g, f32), lam=np.asarray(diff_lambda, f32).reshape(DEPTH, 256),
        dnorm=np.asarray(diff_norm_g, f32),
        wma=np.ascontiguousarray(w_in[:, :, OFF_MA:OFF_MA + D]), wmb=np.ascontiguousarray(w_in[:, :, OFF_MB:OFF_MB + D]),
        wba=np.asarray(w_branch_a, f32), wbb=np.asarray(w_branch_b, f32), wo=np.asarray(w_out, f32))
    per_g = []
    for g in range(4):
        per_g.append(dict(
            wa=np.stack([host_wa(w_in[l], g) for l in range(DEPTH)], axis=0),
            wd=np.stack([host_wd(w_in[l], g) for l in range(DEPTH)], axis=0),
            w2g=np.ascontiguousarray(np.asarray(gla_gk_w2, f32)[:, :, g * 256:(g + 1) * 256]),
            gbias=np.ascontiguousarray(np.asarray(gla_gk_b, f32)[:, g * 256:(g + 1) * 256])))
    ins = []
    for c, (b, g) in enumerate(bg):
        d = dict(shared)
        d.update(per_g[g])
        d["x"] = np.ascontiguousarray(x[b, g * TL:(g + 1) * TL, :])
        ins.append(d)
    res = run_bass_kernel_spmd(nc, ins, core_ids=cores)
    out = np.empty((2, S, D), f32)
    for c, (b, g) in enumerate(bg):
        out[b, g * TL:(g + 1) * TL, :] = np.asarray(res.results[c]["xout"])
    return out


def kernel(**inputs):
    return kernel_fused(**inputs)
```
